# Optimizing a Trainium2 kernel written in Bass

```python
import math
import jax, jax.numpy as jnp
from jax import lax
import numpy as np

D_MODEL = 1024
BATCH = 4
SEQ = 8192
DEPTH = 1

DA_HEADS = 8
DA_HEAD_DIM = 64
DA_V_DIM = 2 * DA_HEAD_DIM
DA_QK_W = DA_HEADS * 2 * DA_HEAD_DIM
DA_V_W = DA_HEADS * DA_V_DIM
Q_BLOCK = 128
DIL_PAIRS = ((128, 1), (512, 4), (2048, 16))
DIL_HEADS_PER_GROUP = 4
DIL_HEADS = DIL_HEADS_PER_GROUP * len(DIL_PAIRS)
DIL_HEAD_DIM = 64
DIL_W = DIL_HEADS * DIL_HEAD_DIM
DIL_OUT_W = DIL_HEADS_PER_GROUP * DIL_HEAD_DIM
N_BRANCH = 2
IN_COLS = 2 * DA_QK_W + DA_V_W + 3 * DIL_W + N_BRANCH * D_MODEL
ROPE_THETA = 500000.0
ROT_DIM = 16
D_FF = 2816
EPS = 1e-6
NEG = -1e30

kernel_name = "hybrid_diffattn_dilated_macaron_block"


def rmsnorm(x, g):
    xf = x.astype(jnp.float32)
    y = xf * lax.rsqrt(jnp.mean(xf * xf, axis=-1, keepdims=True) + EPS)
    return (y * g.astype(jnp.float32)).astype(x.dtype)


def rope_tables(positions, dtype):
    inv = ROPE_THETA ** (-(jnp.arange(0, ROT_DIM, 2, dtype=jnp.float32) / ROT_DIM))
    ang = positions.astype(jnp.float32)[..., None] * inv
    return jnp.cos(ang)[:, :, None, :].astype(dtype), jnp.sin(ang)[:, :, None, :].astype(dtype)


def partial_rope(t, cos, sin):
    half = ROT_DIM // 2
    t1, t2, tp = t[..., :half], t[..., half:ROT_DIM], t[..., ROT_DIM:]
    return jnp.concatenate([t1 * cos - t2 * sin, t2 * cos + t1 * sin, tp], axis=-1)


def swiglu(h, w_gu, w_down):
    g, u = jnp.split(h @ w_gu, 2, axis=-1)
    return (jax.nn.silu(g) * u) @ w_down


def diff_attention(q, k, v, lam):
    B, S, H, _, dh = q.shape
    nblk = S // Q_BLOCK
    scale = 1.0 / math.sqrt(dh)
    qb = q.reshape(B, nblk, Q_BLOCK, H, 2, dh).transpose(1, 0, 2, 3, 4, 5)

    def block(qblk):
        s = jnp.einsum('bqhcd,bkhcd->bhcqk', qblk, k, preferred_element_type=jnp.float32) * scale
        p = jax.nn.softmax(s, axis=-1)
        a = p[:, :, 0] - lam * p[:, :, 1]
        return jnp.einsum('bhqk,bkhe->bqhe', a.astype(v.dtype), v,
                          preferred_element_type=jnp.float32).astype(v.dtype)

    o = lax.map(block, qb)
    return o.transpose(1, 0, 2, 3, 4).reshape(B, S, H, 2 * dh)


def dilated_window_attention(q, k, v, dil, half):
    B, S, Hg, dh = q.shape
    L = S // dil
    N = B * dil
    blk = half
    nb = -(-L // blk)
    Lp = nb * blk
    scale = 1.0 / math.sqrt(dh)

    def fold(t):
        return t.reshape(B, L, dil, Hg, dh).transpose(0, 2, 1, 3, 4).reshape(N, L, Hg, dh)

    qf, kf, vf = fold(q), fold(k), fold(v)
    qb = jnp.pad(qf, ((0, 0), (0, Lp - L), (0, 0), (0, 0))).reshape(N, nb, blk, Hg, dh)

    def ctx(t):
        tp = jnp.pad(t, ((0, 0), (blk, Lp - L + blk), (0, 0), (0, 0))).reshape(N, nb + 2, blk, Hg, dh)
        return jnp.concatenate([tp[:, :-2], tp[:, 1:-1], tp[:, 2:]], axis=2)

    kc, vc = ctx(kf), ctx(vf)
    s = jnp.einsum('nbqhd,nbkhd->nbhqk', qb, kc, preferred_element_type=jnp.float32) * scale
    qpos = jnp.arange(nb)[:, None] * blk + jnp.arange(blk)[None, :]
    kpos = (jnp.arange(nb)[:, None] - 1) * blk + jnp.arange(3 * blk)[None, :]
    valid = ((jnp.abs(kpos[:, None, :] - qpos[:, :, None]) <= half)
             & (kpos >= 0)[:, None, :] & (kpos < L)[:, None, :])
    s = jnp.where(valid[None, :, None], s, NEG)
    m = jnp.max(s, axis=-1, keepdims=True)
    e = jnp.exp(s - m)
    l = jnp.sum(e, axis=-1)
    o = jnp.einsum('nbhqk,nbkhd->nbqhd', e.astype(v.dtype), vc, preferred_element_type=jnp.float32)
    o = o / l.transpose(0, 1, 3, 2)[..., None]
    lse = (m[..., 0] + jnp.log(l)).transpose(0, 1, 3, 2)

    def unfold(t):
        t = t.reshape((N, Lp) + t.shape[3:])[:, :L]
        t = t.reshape((B, dil, L) + t.shape[2:])
        return jnp.swapaxes(t, 1, 2).reshape((B, S) + t.shape[3:])

    return unfold(o).astype(q.dtype), unfold(lse)


def setup_inputs(seed: int = 0) -> dict:
    key = jax.random.key(seed)
    ks = jax.random.split(key, 24)
    f32 = jnp.float32

    def w(k, shape, fan_in):
        return jax.random.normal(k, shape, f32) * fan_in ** -0.5

    def gain(k, shape):
        return 1.0 + 0.05 * jax.random.normal(k, shape, f32)

    return {
        "x": jax.random.normal(ks[0], (BATCH, SEQ, D_MODEL), f32),
        "positions": jnp.broadcast_to(jnp.arange(SEQ, dtype=jnp.int32), (BATCH, SEQ)),
        "w_in": w(ks[1], (DEPTH, D_MODEL, IN_COLS), D_MODEL),
        "lambda_q1": 0.1 * jax.random.normal(ks[2], (DEPTH, DA_HEAD_DIM), f32),
        "lambda_k1": 0.1 * jax.random.normal(ks[3], (DEPTH, DA_HEAD_DIM), f32),
        "lambda_q2": 0.1 * jax.random.normal(ks[4], (DEPTH, DA_HEAD_DIM), f32),
        "lambda_k2": 0.1 * jax.random.normal(ks[5], (DEPTH, DA_HEAD_DIM), f32),
        "g_subln": gain(ks[6], (DEPTH, DA_V_DIM)),
        "w_proj_a": w(ks[7], (DEPTH, DA_V_W, D_MODEL), DA_V_W),
        "w_proj_b": w(ks[8], (DEPTH, DIL_OUT_W, D_MODEL), DIL_OUT_W),
        "w_out": w(ks[9], (DEPTH, D_MODEL, D_MODEL), D_MODEL),
        "w_gu1": w(ks[10], (DEPTH, D_MODEL, 2 * D_FF), D_MODEL),
        "w_down1": w(ks[11], (DEPTH, D_FF, D_MODEL), D_FF),
        "w_gu2": w(ks[12], (DEPTH, D_MODEL, 2 * D_FF), D_MODEL),
        "w_down2": w(ks[13], (DEPTH, D_FF, D_MODEL), D_FF),
        "g_pre_ffn1": gain(ks[14], (DEPTH, D_MODEL)),
        "g_post_ffn1": gain(ks[15], (DEPTH, D_MODEL)),
        "g_pre_mix": gain(ks[16], (DEPTH, D_MODEL)),
        "g_post_mix": gain(ks[17], (DEPTH, D_MODEL)),
        "g_pre_ffn2": gain(ks[18], (DEPTH, D_MODEL)),
        "g_post_ffn2": gain(ks[19], (DEPTH, D_MODEL)),
    }


def reference(x, positions, w_in, lambda_q1, lambda_k1, lambda_q2, lambda_k2, g_subln,
              w_proj_a, w_proj_b, w_out, w_gu1, w_down1, w_gu2, w_down2,
              g_pre_ffn1, g_post_ffn1, g_pre_mix, g_post_mix, g_pre_ffn2, g_post_ffn2):
    B, S, D = x.shape
    cos, sin = rope_tables(positions, x.dtype)
    cuts = [int(c) for c in np.cumsum([DA_QK_W, DA_QK_W, DA_V_W, DIL_W, DIL_W, DIL_W, D_MODEL])]

    for l in range(DEPTH):
        lambda_init = 0.8 - 0.6 * math.exp(-0.3 * l)

        h = rmsnorm(x, g_pre_ffn1[l])
        x = x + 0.5 * rmsnorm(swiglu(h, w_gu1[l], w_down1[l]), g_post_ffn1[l])

        h = rmsnorm(x, g_pre_mix[l])
        z = h @ w_in[l]
        qa, ka, va, qd, kd, vd, ga, gb = jnp.split(z, cuts, axis=-1)

        qa = partial_rope(qa.reshape(B, S, 2 * DA_HEADS, DA_HEAD_DIM), cos, sin)
        ka = partial_rope(ka.reshape(B, S, 2 * DA_HEADS, DA_HEAD_DIM), cos, sin)
        qa = qa.reshape(B, S, DA_HEADS, 2, DA_HEAD_DIM)
        ka = ka.reshape(B, S, DA_HEADS, 2, DA_HEAD_DIM)
        va = va.reshape(B, S, DA_HEADS, DA_V_DIM)
        lam = (jnp.exp(jnp.dot(lambda_q1[l], lambda_k1[l]).astype(jnp.float32))
               - jnp.exp(jnp.dot(lambda_q2[l], lambda_k2[l]).astype(jnp.float32)) + lambda_init)
        oa = diff_attention(qa, ka, va, lam)
        oa = (rmsnorm(oa, g_subln[l]) * (1.0 - lambda_init)).reshape(B, S, DA_V_W)

        qd = partial_rope(qd.reshape(B, S, DIL_HEADS, DIL_HEAD_DIM), cos, sin)
        kd = partial_rope(kd.reshape(B, S, DIL_HEADS, DIL_HEAD_DIM), cos, sin)
        vd = vd.reshape(B, S, DIL_HEADS, DIL_HEAD_DIM)
        outs, lses = [], []
        for gi, (win, dil) in enumerate(DIL_PAIRS):
            hs = slice(gi * DIL_HEADS_PER_GROUP, (gi + 1) * DIL_HEADS_PER_GROUP)
            o_g, lse_g = dilated_window_attention(qd[:, :, hs], kd[:, :, hs], vd[:, :, hs],
                                                  dil, win // (2 * dil))
            outs.append(o_g)
            lses.append(lse_g)
        wts = jax.nn.softmax(jnp.stack(lses, axis=0), axis=0)
        od = jnp.sum(wts[..., None].astype(x.dtype) * jnp.stack(outs, axis=0), axis=0)
        od = od.reshape(B, S, DIL_OUT_W)

        merged = jax.nn.sigmoid(ga) * (oa @ w_proj_a[l]) + jax.nn.sigmoid(gb) * (od @ w_proj_b[l])
        x = x + rmsnorm(merged @ w_out[l], g_post_mix[l])

        h = rmsnorm(x, g_pre_ffn2[l])
        x = x + 0.5 * rmsnorm(swiglu(h, w_gu2[l], w_down2[l]), g_post_ffn2[l])

    return x
```

```python
import math
import numpy as np
import concourse.bass as bass
import concourse.mybir as mybir
from concourse.bass_utils import run_bass_kernel_spmd

F32 = mybir.dt.float32
BF16 = mybir.dt.bfloat16
I32 = mybir.dt.int32
AF = mybir.ActivationFunctionType
ALU = mybir.AluOpType
AX = mybir.AxisListType

S = 8192
D = 1024
SO = 4096
TT = 512
NFC = 22
EPS = 1e-6
THETA = 500000.0
SB_BASE = 16512
SB_END = 229376


class Tok:
    __slots__ = ("w", "wd", "r", "rd")

    def __init__(self):
        self.w = {}
        self.wd = []
        self.r = {}
        self.rd = []


class Op:
    __slots__ = ("eng", "fn", "deps", "sig", "sem", "val", "dma")

    def __init__(self, eng, fn, deps, dma):
        self.eng, self.fn, self.deps, self.dma = eng, fn, deps, dma
        self.sig = False
        self.sem = None
        self.val = 0


class Prog:
    ENGS = ("pe", "act", "dve", "pool", "sp")
    SEM_LIMIT = 30000

    def __init__(self, nc):
        self.nc = nc
        self.ops = {e: [] for e in self.ENGS}
        self.all = []
        self.nsem = 0
        self.last = {}
        self.dmas = []
        self.pending = {e: [] for e in self.ENGS}

    def barrier(self):
        allp = list(self.last.values()) + self.dmas
        self.dmas = []
        for e in self.ENGS:
            self.pending[e] = list(allp)

    def add(self, eng, fn, r=(), w=(), dma=None, deps=(), nobar=False, nowaw=False):
        dl = [d for d in deps if d is not None]
        if self.pending[eng]:
            dl.extend(self.pending[eng])
            self.pending[eng] = []
        for t in r:
            dl.extend(t.w.values())
            dl.extend(t.wd)
        for t in w:
            if not nowaw:
                dl.extend(t.w.values())
                dl.extend(t.wd)
            dl.extend(t.r.values())
            dl.extend(t.rd)
        if eng == "pe":
            dl = [d for d in dl if d.dma is not None or d.eng != "pe"]
        elif dma is None:
            n = len(self.ops[eng])
            recent = set(id(o) for o in self.ops[eng][max(0, n - 2):])
            dl = [d for d in dl if d.dma is not None or d.eng != eng or id(d) in recent]
        op = Op(eng, fn, dl, dma)
        for d in dl:
            d.sig = True
        for t in w:
            if t.r or t.rd:
                t.w, t.wd, t.r, t.rd = {}, [], {}, []
            if dma is None:
                t.w[eng] = op
            else:
                t.wd.append(op)
        for t in r:
            if dma is None:
                t.r[eng] = op
            else:
                t.rd.append(op)
        self.ops[eng].append(op)
        self.all.append(op)
        if dma is None:
            self.last[eng] = op
        elif not nobar:
            self.dmas.append(op)
        return op

    def finalize(self):
        cur = {}
        for op in self.all:
            if op.dma is not None:
                key, inc, need = ("dma", op.dma), 16, True
            else:
                key, inc, need = ("eng", op.eng), 1, op.sig
            if not need:
                continue
            if key not in cur or cur[key][1] + inc > self.SEM_LIMIT:
                self.nsem += 1
                cur[key] = [self.nc.alloc_semaphore(f"s{self.nsem}"), 0]
            cur[key][1] += inc
            op.sem, op.val = cur[key]

    def replay(self, eng, e):
        waited = {}
        for op in self.ops[eng]:
            for d in op.deps:
                k = id(d.sem)
                if waited.get(k, 0) < d.val:
                    e.wait_ge(d.sem, d.val)
                    waited[k] = d.val
            ins = op.fn(e)
            if op.sem is not None:
                ins.then_inc(op.sem, 16 if op.dma is not None else 1)

    def emit(self, final_ops):
        for d in final_ops:
            d.sig = True
        self.finalize()
        with self.nc.Block() as block:
            @block.tensor
            def _(e):
                self.replay("pe", e)

            @block.scalar
            def _(e):
                self.replay("act", e)

            @block.vector
            def _(e):
                self.replay("dve", e)

            @block.gpsimd
            def _(e):
                self.replay("pool", e)

            @block.sync
            def _(e):
                self.replay("sp", e)
                done = {}
                for d in final_ops:
                    if done.get(id(d.sem), 0) < d.val:
                        done[id(d.sem)] = d.val
                        e.wait_ge(d.sem, d.val)


class Arena:
    def __init__(self, nc):
        self.nc = nc
        self.off = SB_BASE
        self.n = 0

    def alloc(self, shape, dtype):
        esz = 2 if dtype == BF16 else 4
        nbytes = int(np.prod(shape[1:])) * esz
        self.off = (self.off + 63) // 64 * 64
        self.n += 1
        t = self.nc.alloc_sbuf_tensor_at(f"t{self.n}", list(shape), dtype, offset=self.off)
        self.off += nbytes
        assert self.off <= SB_END, f"SBUF overflow {self.off}"
        return t


def build_program(cfg=None):
    base = dict(B2m=5, B2g=(0, 1, 2), B2c=True, A_tiles=16, B2=True, B1_heads=8, B1_qc=8, C_tiles=8, dbg=False)
    base.update(cfg or {})
    cfg = base
    dbg = cfg["dbg"]
    nc = bass.Bass("TRN2", target_bir_lowering=False)
    P = Prog(nc)

    def din(name, shape, dt=F32):
        return nc.dram_tensor(name, list(shape), dt, kind="ExternalInput").ap()

    def dscr(name, shape, dt=BF16):
        kind = "ExternalOutput" if (dbg and not name.startswith("w")) else "Internal"
        return nc.dram_tensor(name, list(shape), dt, kind=kind).ap()

    x_in = din("x_loc", [S, D])
    pos_in = din("pos_gf", [128, 512], I32)
    ident_in = din("ident", [128, 128])
    invsgn_in = din("invsgn", [128, 2])
    masks_in = din("masks", [128, 3 * 1024])
    gcols_in = din("gcols", [128, 48])
    gsub_in = din("gsub", [128, 1])
    lam_in = din("lam4", [128, 256])
    WSPEC = [("wgu1", 11, 4096), ("wd1", 8, 2816), ("wrope", 28, 2048), ("wv", 7, 2048),
             ("wg", 8, 2048), ("wpa", 4, 2048), ("wpb", 1, 2048), ("wo", 4, 2048),
             ("wgu2", 11, 4096), ("wd2", 8, 2816)]
    w_in_ap, w_s, w_tok = {}, {}, {}
    for nm, nb, xx in WSPEC:
        w_in_ap[nm] = din(nm, [nb, 128, xx])
        w_s[nm] = dscr(nm + "_s", [nb, 128, xx])
        w_tok[nm] = Tok()
    out_ap = nc.dram_tensor("out", [SO, D], F32, kind="ExternalOutput").ap()

    cs_c = dscr("cs_c", [3, 16, 8, 512], F32)
    kT_s = dscr("kT_s", [14, 128, S])
    qT_s = dscr("qT_s", [14, 128, SO])
    v_s = dscr("v_s", [S, 1792])
    x1_s = dscr("x1_s", [8, 128, 8 * TT], F32)
    h2_s = dscr("h2_s", [8, 128, 8 * TT])
    oa_s = dscr("oa_s", [8, 128, SO])
    od_s = dscr("od_s", [2, 128, SO])
    T_cs, T_kT, T_qT, T_v, T_x1, T_h2, T_oa, T_od = (Tok() for _ in range(8))

    ps = nc.alloc_psum_tensor("ps", [128, 8, 512], F32)
    PB = [Tok() for _ in range(8)]

    A = Arena(nc)
    ident = A.alloc([128, 128], F32)
    ones_bf = A.alloc([128, 128], BF16)
    ones_f = A.alloc([128, 128], F32)
    masks = A.alloc([128, 3, 1024], F32)
    gcols = A.alloc([128, 48], F32)
    ghalf = A.alloc([128, 16], F32)
    gsub8 = A.alloc([128, 1], F32)
    neglam = A.alloc([128, 1], F32)
    invsgn = A.alloc([128, 2], F32)
    T_const = Tok()
    const_mark = A.off

    def MM(out, lhsT, rhs, start, stop, r, w, tp=None):
        if tp is None:
            return P.add("pe", lambda e: e.matmul(out, lhsT=lhsT, rhs=rhs, start=start, stop=stop), r=r, w=w)
        return P.add("pe", lambda e: e.matmul(out, lhsT=lhsT, rhs=rhs, start=start, stop=stop,
                                              tile_position=tp), r=r, w=w)

    def TR(out, in_, r, w):
        return P.add("pe", lambda e: e.transpose(out, in_, ident[:]), r=list(r) + [T_const], w=w)

    def ACT(out, in_, func, r, w, scale=1.0, bias=0.0):
        return P.add("act", lambda e: e.activation(out=out, in_=in_, func=func, bias=bias, scale=scale), r=r, w=w)

    def CP(eng, out, in_, r, w):
        return P.add(eng, lambda e: e.tensor_copy(out, in_), r=r, w=w)

    def TTo(eng, out, in0, in1, op, r, w):
        return P.add(eng, lambda e: e.tensor_tensor(out=out, in0=in0, in1=in1, op=op), r=r, w=w)

    def TS(eng, out, in0, s1, s2, op0, op1, r, w):
        if s2 is None:
            return P.add(eng, lambda e: e.tensor_scalar(out=out, in0=in0, scalar1=s1, scalar2=None, op0=op0), r=r, w=w)
        return P.add(eng, lambda e: e.tensor_scalar(out=out, in0=in0, scalar1=s1, scalar2=s2, op0=op0, op1=op1), r=r, w=w)

    def STT(eng, out, in0, sc, in1, op0, op1, r, w):
        return P.add(eng, lambda e: e.scalar_tensor_tensor(out=out, in0=in0, scalar=sc, in1=in1, op0=op0, op1=op1),
                     r=r, w=w)

    def RCP(out, in_, r, w):
        return P.add("dve", lambda e: e.reciprocal(out, in_), r=r, w=w)

    def DMA(eng, out, in_, key, r, w, nobar=False, nowaw=False, deps=()):
        return P.add(eng, lambda e: e.dma_start(out=out, in_=in_), r=r, w=w, dma=key, nobar=nobar, nowaw=nowaw, deps=deps)

    DMA("sp", ident[:], ident_in, "c0", [], [T_const], nowaw=True)
    DMA("sp", masks[:].rearrange("p a b -> p (a b)"), masks_in, "c0", [], [T_const], nowaw=True)
    DMA("sp", gcols[:], gcols_in, "c0", [], [T_const], nowaw=True)
    DMA("sp", invsgn[:], invsgn_in, "c0", [], [T_const], nowaw=True)
    P.add("dve", lambda e: e.memset(ones_bf[:], 1.0), w=[T_const])
    P.add("dve", lambda e: e.memset(ones_f[:], 1.0), w=[T_const])
    w_btok = {}
    w_blk_all = {}
    conv_later = []

    def conv_block(nm, b, deps=()):
        if nm in ("wgu1", "wd1"):
            DMA("pool", w_s[nm][b], w_in_ap[nm][b], f"cv_{nm}_{b}", [], [w_btok[(nm, b)]], nobar=True, deps=deps)
        else:
            DMA("pool", w_s[nm][b], w_in_ap[nm][b], "cv_" + nm, [], [w_blk_all[nm][b]], nobar=True, deps=deps)

    for nm, nb, xx in WSPEC:
        for b in range(nb):
            if nm in ("wgu1", "wd1"):
                w_btok[(nm, b)] = Tok()
            else:
                w_blk_all.setdefault(nm, []).append(Tok())
    T_xtok0 = Tok()
    x0_load = [None]

    setup_mark = A.off
    T_tmp = Tok()
    gs_t = A.alloc([128, 1], F32)
    lam_t = A.alloc([128, 256], F32)
    lpr = A.alloc([128, 128], F32)
    ld = A.alloc([128, 2], F32)
    le = A.alloc([128, 2], F32)
    DMA("sp", gs_t[:], gsub_in, "c1", [], [T_tmp], nowaw=True)
    DMA("sp", lam_t[:], lam_in, "c1", [], [T_tmp], nowaw=True)
    TS("dve", ghalf[:, 0:8], gcols[:, 8:16], 0.5, None, ALU.mult, None, [T_const], [T_const])
    TS("dve", ghalf[:, 8:16], gcols[:, 40:48], 0.5, None, ALU.mult, None, [T_const], [T_const])
    TS("dve", gsub8[:], gs_t[:], 0.8, None, ALU.mult, None, [T_tmp], [T_const])
    T_l = Tok()
    TTo("dve", lpr[:, 0:64], lam_t[:, 0:64], lam_t[:, 64:128], ALU.mult, [T_tmp], [T_l])
    TTo("dve", lpr[:, 64:128], lam_t[:, 128:192], lam_t[:, 192:256], ALU.mult, [T_tmp], [T_l])
    P.add("dve", lambda e: e.tensor_reduce(out=ld[:, 0:1], in_=lpr[:, 0:64], axis=AX.X, op=ALU.add), r=[T_l], w=[T_l])
    P.add("dve", lambda e: e.tensor_reduce(out=ld[:, 1:2], in_=lpr[:, 64:128], axis=AX.X, op=ALU.add), r=[T_l], w=[T_l])
    ACT(le[:], ld[:], AF.Exp, [T_l], [T_l])
    TTo("dve", ld[:, 0:1], le[:, 1:2], le[:, 0:1], ALU.subtract, [T_l], [T_l])
    TS("dve", neglam[:], ld[:, 0:1], -0.2, None, ALU.add, None, [T_l], [T_const])

    RC = 512
    posi = A.alloc([128, RC], I32)
    posf = A.alloc([128, RC], F32)
    ang = A.alloc([128, RC], F32)
    kf = A.alloc([128, RC], F32)
    ki = A.alloc([128, RC], I32)
    fx = A.alloc([128, RC], F32)
    kn = A.alloc([128, RC], F32)
    T_pos, T_ang, T_k, T_kn = Tok(), Tok(), Tok(), Tok()
    TWO_PI = 2 * math.pi
    cs_flat = cs_c.rearrange("k g f n -> k (g f) n")

    def reduce_sin(shift, kidx, neg_idx):
        TS("dve", fx[:], ang[:], shift, None, ALU.add, None, [T_ang], [T_k])
        TS("dve", kf[:], fx[:], 1.0 / TWO_PI, 0.5, ALU.mult, ALU.add, [T_k], [T_k])
        CP("dve", ki[:], kf[:], [T_k], [T_k])
        CP("dve", kf[:], ki[:], [T_k], [T_k])
        STT("dve", fx[:], kf[:], -TWO_PI, fx[:], ALU.mult, ALU.add, [T_k], [T_k])
        TS("dve", kf[:], fx[:], math.pi, -TWO_PI, ALU.is_gt, ALU.mult, [T_k], [T_k])
        TTo("dve", fx[:], fx[:], kf[:], ALU.add, [T_k], [T_k])
        TS("dve", kf[:], fx[:], -math.pi, TWO_PI, ALU.is_lt, ALU.mult, [T_k], [T_k])
        TTo("dve", fx[:], fx[:], kf[:], ALU.add, [T_k], [T_k])
        TS("dve", fx[:], fx[:], -math.pi, math.pi, ALU.max, ALU.min, [T_k], [T_k])
        ACT(kf[:], fx[:], AF.Sin, [T_k], [T_k])
        DMA("sp", cs_flat[kidx], kf[:], "cs_w", [T_k], [T_cs])
        if neg_idx is not None:
            TS("dve", kn[:], kf[:], -1.0, None, ALU.mult, None, [T_k], [T_kn])
            DMA("sp", cs_flat[neg_idx], kn[:], "cs_w", [T_kn], [T_cs])

    DMA("sp", posi[:], pos_in, "pos", [], [T_pos])
    CP("dve", posf[:], posi[:], [T_pos], [T_pos])
    TS("dve", ang[:], posf[:], invsgn[:, 0:1], None, ALU.mult, None, [T_pos, T_const], [T_ang])
    reduce_sin(math.pi / 2, 0, None)
    reduce_sin(0.0, 2, 1)

    P.barrier()
    A.off = setup_mark
    xtok = A.alloc([128, 4, 1024], F32)
    xT = A.alloc([128, 8, TT], F32)
    hT = A.alloc([128, 8, TT], BF16)
    sq = A.alloc([128, 8, TT], BF16)
    aT = A.alloc([128, NFC, TT], BF16)
    yT = A.alloc([128, 8, TT], F32)
    rs = A.alloc([128, TT], F32)
    tmpa = [A.alloc([128, TT], F32) for _ in range(2)]
    sg = [A.alloc([128, TT], F32) for _ in range(2)]
    cs2 = [A.alloc([128, 2, TT], F32) for _ in range(2)]
    T_cs2 = [Tok(), Tok()]
    kst = A.alloc([128, 14, TT], BF16)
    vst = A.alloc([128, 4, 1792], BF16)
    NW = 5
    wpool = [A.alloc([128, 4096], BF16) for _ in range(NW)]
    h2T = [A.alloc([128, 8, TT], BF16) for _ in range(2)]
    T_h2c = [[Tok() for _ in range(8)] for _ in range(2)]
    tmpb = [A.alloc([128, TT], F32) for _ in range(2)]
    T_tmpb = [Tok(), Tok()]
    T_xtok, T_xT, T_hT, T_sq, T_yT, T_rs, T_cst, T_kst, T_vst, T_oaT, T_odT, T_mg = (Tok() for _ in range(12))
    T_aT = [Tok() for _ in range(NFC)]
    T_tmpa = [Tok(), Tok()]
    T_sg = [Tok(), Tok()]
    T_w = [Tok() for _ in range(NW)]
    wctr = [0]
    tctr = [0]

    def wload(nm, b, nelem):
        i = wctr[0] % NW
        wctr[0] += 1
        rd = [w_btok[(nm, b)]] if (nm, b) in w_btok else w_blk_all[nm]
        DMA("sp", wpool[i][:, 0:nelem], w_s[nm][b][:, 0:nelem], f"w{i}", rd, [T_w[i]])
        return wpool[i], T_w[i]

    T_sqc = [Tok() for _ in range(8)]
    T_yc = [Tok() for _ in range(8)]
    T_xc = [Tok() for _ in range(8)]
    T_hc = [Tok() for _ in range(8)]

    T_ser = Tok()

    class Stats:
        def __init__(self):
            self.pending = None
            self.n = 0

        def push(self, c):
            self.flush()
            self.pending = c

        def flush(self):
            if self.pending is not None:
                c = self.pending
                MM(ps[:, 1, :], ones_bf[:], sq[:, c, :], self.n == 0, self.n == 7, [T_sqc[c], T_const], [PB[1]])
                self.n += 1
                self.pending = None

        def finish(self, n_feat):
            self.flush()
            assert self.n == 8
            ACT(rs[:], ps[:, 1, :], AF.Ln, [PB[1]], [T_rs], scale=1.0 / n_feat, bias=EPS)
            ACT(rs[:], rs[:], AF.Exp, [T_rs], [T_rs], scale=-0.5)

    def h_from_x(gbase):
        for c in range(8):
            eng = "dve"
            STT(eng, hT[:, c, :], xT[:, c, :], gcols[:, gbase + c:gbase + c + 1], rs[:], ALU.mult, ALU.mult,
                [T_xc[c], T_rs, T_const], [T_hc[c]])

    def resid_apply(next_gbase):
        st = Stats()
        for c in (6, 7):
            TTo("pool", yT[:, c, :], yT[:, c, :], rs[:], ALU.mult, [T_yc[c], T_rs], [T_yc[c]])
            TTo("pool", xT[:, c, :], xT[:, c, :], yT[:, c, :], ALU.add, [T_xc[c], T_yc[c]], [T_xc[c]])
        for c in range(6):
            k = tctr[0] % 2
            tctr[0] += 1
            TTo("dve", tmpa[k][:], yT[:, c, :], rs[:], ALU.mult, [T_yc[c], T_rs], [T_tmpa[k]])
            TTo("dve", xT[:, c, :], xT[:, c, :], tmpa[k][:], ALU.add, [T_xc[c], T_tmpa[k]], [T_xc[c]])
            if next_gbase is not None:
                ACT(sq[:, c, :], xT[:, c, :], AF.Square, [T_xc[c]], [T_sqc[c]])
                st.push(c)
        for c in (6, 7):
            if next_gbase is not None:
                ACT(sq[:, c, :], xT[:, c, :], AF.Square, [T_xc[c]], [T_sqc[c]])
                st.push(c)
        if next_gbase is not None:
            st.finish(D)
            h_from_x(next_gbase)

    def evac_y(o, bank, gtile, gbase, st):
        ACT(sq[:, o, :], ps[:, bank, :], AF.Square, [PB[bank]], [T_sqc[o], T_ser])
        st.push(o)
        TS("dve", yT[:, o, :], ps[:, bank, :], gtile[:, gbase + o:gbase + o + 1], None, ALU.mult, None,
           [PB[bank], T_const, T_ser], [T_yc[o]])

    def ffn(wgu, wd, gtile, gbase, mid=None):
        for blk in range(11):
            if blk == 6 and mid is not None:
                mid()
            wb, tw = wload(wgu, blk, 4096)
            for fl in range(2):
                f = blk * 2 + fl
                st = f % 2
                bg, bu = 2 + 2 * st, 3 + 2 * st
                for gu, bank in ((0, bg), (1, bu)):
                    for kc in range(8):
                        o0 = kc * 512 + gu * 256 + fl * 128
                        MM(ps[:, bank, :], wb[:, o0:o0 + 128], hT[:, kc, :], kc == 0, kc == 7, [tw, T_hc[kc]], [PB[bank]])
                ACT(sg[st][:], ps[:, bg, :], AF.Silu, [PB[bg]], [T_sg[st]])
                TTo("dve", aT[:, f, :], sg[st][:], ps[:, bu, :], ALU.mult, [T_sg[st], PB[bu]], [T_aT[f]])
        stt = Stats()
        for o in range(8):
            wb, tw = wload(wd, o, 2816)
            bank = 6 + o % 2
            for f in range(NFC):
                MM(ps[:, bank, :], wb[:, f * 128:(f + 1) * 128], aT[:, f, :], f == 0, f == NFC - 1,
                   [tw, T_aT[f]], [PB[bank]])
            evac_y(o, bank, gtile, gbase, stt)
        stt.finish(D)

    def rope_proj(blocks, dst_scr, T_dst, t, hsrc, T_hs, csb, T_csb, slot):
        for i, b in enumerate(blocks):
            wb, tw = wload("wrope", b, 2048)
            st = i % 2
            bp, br = 2 + 2 * st, 3 + 2 * st
            for rot, bank in ((0, bp), (1, br)):
                for kc in range(8):
                    o0 = kc * 256 + rot * 128
                    MM(ps[:, bank, :], wb[:, o0:o0 + 128], hsrc[:, kc, :], kc == 0, kc == 7, [tw, T_hs[kc]], [PB[bank]])
            TTo("dve", tmpb[0][:], ps[:, bp, :], csb[:, 0, :], ALU.mult, [PB[bp], T_csb], [T_tmpb[0]])
            TTo("dve", tmpb[1][:], ps[:, br, :], csb[:, 1, :], ALU.mult, [PB[br], T_csb], [T_tmpb[1]])
            TTo("pool", kst[:, i, :], tmpb[0][:], tmpb[1][:], ALU.add, [T_tmpb[0], T_tmpb[1]], [T_kst])
            slot()
        DMA("pool", dst_scr[:, :, t * TT:(t + 1) * TT].rearrange("c p n -> p c n"), kst[:], "kst_w", [T_kst], [T_dst])

    def v_proj(t, hsrc, T_hs, slot):
        bsel = [2, 3, 4, 5]
        cnt = 0
        for pair in range(4):
            vbs = [2 * pair, 2 * pair + 1] if pair < 3 else [6]
            wbs = [wload("wv", vb, 2048) for vb in vbs]
            for j in range(4):
                bank = bsel[cnt % 4]
                cnt += 1
                for q_, (wb, tw) in enumerate(wbs):
                    for kc in range(8):
                        MM(ps[:, bank, q_ * 256:(q_ + 1) * 256], hsrc[:, kc, j * 128:(j + 1) * 128],
                           wb[:, kc * 256:(kc + 1) * 256], kc == 0, kc == 7, [tw, T_hs[kc]], [PB[bank]])
                ncol = 256 * len(vbs)
                CP("dve", vst[:, j, pair * 512:pair * 512 + ncol], ps[:, bank, 0:ncol], [PB[bank]], [T_vst])
                slot()
        DMA("pool", v_s[t * TT:(t + 1) * TT, :].rearrange("(j p) c -> p j c", p=128), vst[:], "vst_w", [T_vst], [T_v])

    def chain_thunks(t, own, h2dst, T_h2d):
        st = Stats()
        q = []
        for c in (6, 7):
            q.append(lambda c=c: TTo("pool", yT[:, c, :], yT[:, c, :], rs[:], ALU.mult, [T_yc[c], T_rs], [T_yc[c]]))
            q.append(lambda c=c: TTo("pool", xT[:, c, :], xT[:, c, :], yT[:, c, :], ALU.add, [T_xc[c], T_yc[c]], [T_xc[c]]))
        for c in range(6):
            k = c % 2
            q.append(lambda c=c, k=k: TTo("dve", tmpa[k][:], yT[:, c, :], rs[:], ALU.mult, [T_yc[c], T_rs], [T_tmpa[k]]))
            q.append(lambda c=c, k=k: TTo("dve", xT[:, c, :], xT[:, c, :], tmpa[k][:], ALU.add, [T_xc[c], T_tmpa[k]], [T_xc[c]]))
            q.append(lambda c=c: ACT(sq[:, c, :], xT[:, c, :], AF.Square, [T_xc[c]], [T_sqc[c]]))
            q.append(lambda c=c: st.push(c))
        for c in (6, 7):
            q.append(lambda c=c: ACT(sq[:, c, :], xT[:, c, :], AF.Square, [T_xc[c]], [T_sqc[c]]))
            q.append(lambda c=c: st.push(c))
        q.append(lambda: st.finish(D))
        for c in range(8):
            q.append(lambda c=c: STT("dve", h2dst[:, c, :], xT[:, c, :], gcols[:, 16 + c:17 + c], rs[:], ALU.mult, ALU.mult,
                                     [T_xc[c], T_rs, T_const], [T_h2d[c]]))
        if own:
            q.append(lambda: DMA("pool", x1_s[t], xT[:].rearrange("p c n -> p (c n)"), "x1_w", T_xc, [T_x1]))
            q.append(lambda: DMA("pool", h2_s[t], h2dst[:].rearrange("p c n -> p (c n)"), "h2_w", T_h2d, [T_h2]))
        return q

    for cb in cs2:
        P.add("dve", lambda e, cb=cb: e.memset(cb[:, 0, :], 1.0), w=[T_cs2[0], T_cs2[1]])
        P.add("dve", lambda e, cb=cb: e.memset(cb[:, 1, :], 0.0), w=[T_cs2[0], T_cs2[1]])
    a_tiles = list(cfg["A_tiles"]) if isinstance(cfg["A_tiles"], (list, tuple)) else list(range(cfg["A_tiles"]))
    prev = None

    def run_proj(pt, pown, pb, q):
        def slot():
            for _ in range(2):
                if q:
                    q.pop(0)()
        rope_proj(list(range(0, 14)), kT_s, T_kT, pt, h2T[pb], T_h2c[pb], cs2[pb], T_cs2[pb], slot)
        if pown:
            rope_proj(list(range(14, 28)), qT_s, T_qT, pt, h2T[pb], T_h2c[pb], cs2[pb], T_cs2[pb], slot)
        v_proj(pt, h2T[pb], T_h2c[pb], slot)
        while q:
            q.pop(0)()

    def x_load(tt):
        return DMA("sp", xtok[:], x_in[tt * TT:(tt + 1) * TT, :].rearrange("(j p) d -> p j d", p=128), "xl", [], [T_xtok])

    xl_op = x_load(a_tiles[0]) if a_tiles else None
    for ti, t in enumerate(a_tiles):
        own = t < 8
        bi = t % 2
        if t == a_tiles[0]:
            for nm, nb, xx in WSPEC:
                for b in range(nb):
                    if nm in ("wgu1", "wd1", "wrope", "wv"):
                        conv_block(nm, b, deps=[xl_op])
                    else:
                        conv_later.append((nm, b))
        else:
            for _ in range(3):
                if conv_later:
                    conv_block(*conv_later.pop(0))
        for hb_ in (0, 64):
            DMA("sp", cs2[bi][hb_:hb_ + 8, 0, :], cs_c[0, t], f"csl{bi}", [T_cs], [T_cs2[bi]])
            DMA("sp", cs2[bi][hb_ + 8:hb_ + 16, 0, :], cs_c[0, t], f"csl{bi}", [T_cs], [T_cs2[bi]])
            DMA("sp", cs2[bi][hb_:hb_ + 8, 1, :], cs_c[1, t], f"csl{bi}", [T_cs], [T_cs2[bi]])
            DMA("sp", cs2[bi][hb_ + 8:hb_ + 16, 1, :], cs_c[2, t], f"csl{bi}", [T_cs], [T_cs2[bi]])
        st0 = Stats()
        for c in range(8):
            bank = 6 + c % 2
            for j in range(4):
                TR(ps[:, bank, j * 128:(j + 1) * 128], xtok[:, j, c * 128:(c + 1) * 128], [T_xtok], [PB[bank]])
            CP("dve", xT[:, c, :], ps[:, bank, :], [PB[bank]], [T_xc[c]])
            ACT(sq[:, c, :], xT[:, c, :], AF.Square, [T_xc[c]], [T_sqc[c]])
            st0.push(c)
        st0.finish(D)
        h_from_x(0)
        nxt = (lambda tn=a_tiles[ti + 1]: x_load(tn)) if ti + 1 < len(a_tiles) else None
        ffn("wgu1", "wd1", ghalf, 0, mid=nxt)
        q = chain_thunks(t, own, h2T[bi], T_h2c[bi])
        if prev is not None:
            run_proj(prev[0], prev[1], prev[2], q)
        else:
            while q:
                q.pop(0)()
        prev = (t, own, bi)
    if prev is not None:
        run_proj(prev[0], prev[1], prev[2], [])

    while conv_later:
        conv_block(*conv_later.pop(0))
    P.barrier()
    A.off = setup_mark
    kd2 = [A.alloc([128, 2, 6144], BF16) for _ in range(2)]
    qd2 = [A.alloc([128, 2, SO], BF16) for _ in range(2)]
    vt = [A.alloc([128, 33, 256], BF16) for _ in range(2)]
    Uacc = A.alloc([128, 2, SO], F32)
    Lacc = A.alloc([128, 2, SO], F32)
    pef = [A.alloc([128, 2, 512], F32) for _ in range(2)]
    pm = [A.alloc([128, 1024], BF16) for _ in range(2)]
    odst = A.alloc([128, 2, 512], BF16)
    T_U, T_L, T_odst = (Tok() for _ in range(3))
    T_kd2 = [Tok(), Tok()]
    T_qd2 = [Tok(), Tok()]
    b2_groups = [(g_, (1, 4, 16)[g_]) for g_ in cfg["B2g"]] if cfg["B2"] else []

    def b2_group_loads(gi_):
        g_, _d = b2_groups[gi_]
        bb = gi_ % 2
        for cc_ in range(2):
            ci_ = 8 + 2 * g_ + cc_
            DMA("sp", kd2[bb][:, cc_, 0:1024], kT_s[ci_, :, S - 1024:S], f"kd_l{bb}", [T_kT], [T_kd2[bb]])
            DMA("sp", kd2[bb][:, cc_, 1024:6144], kT_s[ci_, :, 0:5120], f"kd_l{bb}", [T_kT], [T_kd2[bb]])
            DMA("sp", qd2[bb][:, cc_, :], qT_s[ci_, :, :], f"qd_l{bb}", [T_qT], [T_qd2[bb]])
    T_vt = [Tok(), Tok()]
    T_pef = [Tok(), Tok()]
    T_pm = [Tok(), Tok()]
    def B2_rest(sb, i, r, nq, vtt, tvt, Uv, Lv, g):
        ACT(pef[sb][:], ps[:, 2 * sb:2 * sb + 2, :], AF.Exp, [PB[2 * sb], PB[2 * sb + 1]], [T_pef[sb]], scale=0.125)
        mi = 1 if i == 0 else (2 if i == nq - 1 else 0)
        TTo("dve", pm[sb][:], pef[sb][:].rearrange("p a b -> p (a b)"), masks[:, mi, :], ALU.mult,
            [T_pef[sb], T_const], [T_pm[sb]])
        ub, lb = 4 + 2 * sb, 5 + 2 * sb
        if cfg["B2m"] < 3:
            return
        for j in range(4):
            cc, hp = j // 2, j % 2
            for m_ in range(2):
                blk = hp * 4 + m_ * 2 + cc
                MM(ps[hp * 64:(hp + 1) * 64, ub, cc * 128:(cc + 1) * 128],
                   vtt[:, i + m_, j * 64:(j + 1) * 64], pm[sb][:, blk * 128:(blk + 1) * 128],
                   m_ == 0, m_ == 1, [tvt, T_pm[sb]], [PB[ub]], tp=(0, hp * 64))
            for m_ in (range(2) if cfg["B2m"] >= 4 else []):
                blk = hp * 4 + m_ * 2 + cc
                MM(ps[hp * 64:(hp + 1) * 64, lb, cc * 128:(cc + 1) * 128],
                   ones_bf[:, 0:64], pm[sb][:, blk * 128:(blk + 1) * 128],
                   m_ == 0, m_ == 1, [T_const, T_pm[sb]], [PB[lb]], tp=(0, hp * 64))
        if cfg["B2m"] < 5:
            return
        usrc = ps[:, ub, 0:256].rearrange("p (c n) -> p c n", c=2)
        lsrc = ps[:, lb, 0:256].rearrange("p (c n) -> p c n", c=2)
        udst = Uv[:, :, 128 * i:128 * (i + 1), r]
        ldst = Lv[:, :, 128 * i:128 * (i + 1), r]
        if g == 0:
            CP("dve", udst, usrc, [PB[ub]], [T_U])
            CP("dve", ldst, lsrc, [PB[lb]], [T_L])
        else:
            TTo("dve", udst, udst, usrc, ALU.add, [PB[ub], T_U], [T_U])
            TTo("dve", ldst, ldst, lsrc, ALU.add, [PB[lb], T_L], [T_L])

    pend = [None]
    step = 0
    vcnt = 0
    for gi0 in range(min(2, len(b2_groups))):
        b2_group_loads(gi0)
    for gi, (g, dil) in enumerate(b2_groups):
        if gi >= 1 and gi + 1 < len(b2_groups):
            b2_group_loads(gi + 1)
        kd, qd = kd2[gi % 2], qd2[gi % 2]
        T_kd, T_qd = T_kd2[gi % 2], T_qd2[gi % 2]
        nq = SO // (128 * dil)
        kdv = kd[:].rearrange("p c (n d) -> p c n d", d=dil)
        qdv = qd[:].rearrange("p c (n d) -> p c n d", d=dil)
        Uv = Uacc[:].rearrange("p c (n d) -> p c n d", d=dil)
        Lv = Lacc[:].rearrange("p c (n d) -> p c n d", d=dil)
        colb = 1024 + g * 256
        for r in range(dil):
            vb = vcnt % 2
            vcnt += 1
            vtt, tvt = vt[vb], T_vt[vb]
            rs0 = v_s.tensor
            DMA("sp", vtt[0:64, 0, :], bass.AP(rs0, (S + r - 64 * dil) * 1792 + colb, [[dil * 1792, 64], [1, 256]]),
                f"vt{vb}", [T_v], [tvt])
            DMA("sp", vtt[64:128, 0, :], bass.AP(rs0, r * 1792 + colb, [[dil * 1792, 64], [1, 256]]),
                f"vt{vb}", [T_v], [tvt])
            m0 = 1
            while m0 <= nq:
                m1 = min(nq + 1, m0 + 8)
                DMA("sp", vtt[:, m0:m1, :],
                    bass.AP(rs0, (r + dil * (128 * m0 - 64)) * 1792 + colb,
                            [[dil * 1792, 128], [128 * dil * 1792, m1 - m0], [1, 256]]),
                    f"vt{vb}", [T_v], [tvt])
                m0 = m1
            for i in (range(nq) if cfg["B2c"] else []):
                sb = step % 2
                step += 1
                prev_rest = pend[0]
                for m_ in range(2):
                    kbase = (1024 // dil) + 128 * i - 64 + 128 * m_
                    for j in range(4):
                        cc, hp = j // 2, j % 2
                        blk = hp * 4 + m_ * 2 + cc
                        bank = 2 * sb + hp
                        col = (m_ * 2 + cc) * 128
                        MM(ps[:, bank, col:col + 128],
                           kdv[hp * 64:(hp + 1) * 64, cc, kbase:kbase + 128, r],
                           qdv[hp * 64:(hp + 1) * 64, cc, 128 * i:128 * (i + 1), r],
                           True, True, [T_kd, T_qd], [PB[2 * sb], PB[2 * sb + 1]])
                if prev_rest is not None:
                    prev_rest()

                def rest(sb=sb, i=i, r=r, nq=nq, vtt=vtt, tvt=tvt, Uv=Uv, Lv=Lv, g=g):
                    B2_rest(sb, i, r, nq, vtt, tvt, Uv, Lv, g)
                pend[0] = rest
    if pend[0] is not None:
        pend[0]()
    for q8 in (range(8) if cfg["B2"] else []):
        sl = slice(q8 * 512, (q8 + 1) * 512)
        RCP(Lacc[:, :, sl], Lacc[:, :, sl], [T_L], [T_L])
        TTo("dve", odst[:], Uacc[:, :, sl], Lacc[:, :, sl], ALU.mult, [T_U, T_L], [T_odst])
        DMA("pool", od_s[:, :, sl].rearrange("c p n -> p c n"), odst[:], "od_w", [T_odst], [T_od])

    P.barrier()
    A.off = setup_mark
    kT = [A.alloc([128, S], BF16) for _ in range(2)]
    vv = [A.alloc([128, 64, 128], BF16) for _ in range(2)]
    qT = [A.alloc([128, SO], BF16) for _ in range(2)]
    NP = 4
    pT = [A.alloc([128, 2, 512], BF16) for _ in range(NP)]
    lb1 = A.alloc([128, 512], F32)
    lb2 = A.alloc([128, 512], F32)
    acc1 = [A.alloc([128, 512], F32) for _ in range(2)]
    T_acc = [Tok(), Tok()]
    o1s = A.alloc([128, 512], F32)
    o2s = A.alloc([128, 512], F32)
    l2s = A.alloc([128, 512], F32)
    e1 = A.alloc([128, 512], F32)
    e2 = A.alloc([128, 512], F32)
    osb = A.alloc([128, 512], F32)
    sqo = A.alloc([128, 512], BF16)
    rso = A.alloc([128, 512], F32)
    oast = [A.alloc([128, 512], BF16) for _ in range(2)]
    T_kTb = [Tok(), Tok()]
    T_vv = [Tok(), Tok()]
    T_qTb = [Tok(), Tok()]
    T_pT = [Tok() for _ in range(NP)]
    T_o1s, T_o2s, T_l2s, T_lb1, T_lb2, T_e1, T_e2, T_osb, T_sqo, T_rso = (Tok() for _ in range(10))
    T_oast = [Tok(), Tok()]
    gstep = 0
    ecnt = 0
    epi_q = []

    def epi_pop():
        if epi_q:
            f = epi_q.pop(0)
            if f is not None:
                f()

    for h in range(cfg["B1_heads"]):
        hb = h % 2
        DMA("sp", kT[hb][:], kT_s[h], f"kT{hb}", [T_kT], [T_kTb[hb]])
        DMA("sp", qT[hb][:], qT_s[h], f"qT{hb}", [T_qT], [T_qTb[hb]])
        vsrc = v_s[:, h * 128:(h + 1) * 128].rearrange("(kt p) c -> p kt c", p=128)
        for k4 in range(4):
            DMA("sp", vv[hb][:, k4 * 16:(k4 + 1) * 16, :], vsrc[:, k4 * 16:(k4 + 1) * 16, :], f"vv{hb}", [T_v], [T_vv[hb]])
        for qc in range(cfg["B1_qc"]):
            qsl = slice(qc * 512, (qc + 1) * 512)
            ab = ecnt % 2

            def QK(kt, sb):
                ksl = slice(kt * 128, (kt + 1) * 128)
                MM(ps[:, 2 * sb, :], kT[hb][0:64, ksl], qT[hb][0:64, qsl], True, True,
                   [T_kTb[hb], T_qTb[hb]], [PB[2 * sb], PB[2 * sb + 1]])
                MM(ps[:, 2 * sb + 1, :], kT[hb][64:128, ksl], qT[hb][64:128, qsl], True, True,
                   [T_kTb[hb], T_qTb[hb]], [PB[2 * sb], PB[2 * sb + 1]])

            QK(0, gstep % 2)
            QK(1, (gstep + 1) % 2)
            for kt in range(64):
                sb = gstep % 2
                pi = gstep % NP
                gstep += 1
                ACT(pT[pi][:], ps[:, 2 * sb:2 * sb + 2, :], AF.Exp, [PB[2 * sb], PB[2 * sb + 1]], [T_pT[pi]], scale=0.125)
                if kt + 2 < 64:
                    QK(kt + 2, sb)
                first, last = kt == 0, kt == 63
                MM(ps[:, 4, :], vv[hb][:, kt, :], pT[pi][:, 0, :], first, last, [T_vv[hb], T_pT[pi]], [PB[4]])
                MM(ps[:, 5, :], vv[hb][:, kt, :], pT[pi][:, 1, :], first, last, [T_vv[hb], T_pT[pi]], [PB[5]])
                MM(ps[:, 7, :], ones_bf[:], pT[pi][:, 1, :], first, last, [T_const, T_pT[pi]], [PB[7]])
                if first:
                    CP("dve", acc1[ab][:], pT[pi][:, 0, :], [T_pT[pi]], [T_acc[ab]])
                else:
                    TTo("dve", acc1[ab][:], acc1[ab][:], pT[pi][:, 0, :], ALU.add, [T_pT[pi], T_acc[ab]], [T_acc[ab]])
                epi_pop()
            while epi_q:
                epi_pop()
            CP("dve", o1s[:], ps[:, 4, :], [PB[4]], [T_o1s])
            CP("dve", o2s[:], ps[:, 5, :], [PB[5]], [T_o2s])
            CP("dve", l2s[:], ps[:, 7, :], [PB[7]], [T_l2s])
            MM(ps[:, 6, :], ones_f[:], acc1[ab][:], True, True, [T_acc[ab], T_const], [PB[6]])
            ob = ecnt % 2
            ecnt += 1

            def mk(h=h, qsl=qsl, ob=ob):
                return [
                    lambda: RCP(lb1[:], ps[:, 6, :], [PB[6]], [T_lb1]),
                    lambda: RCP(lb2[:], l2s[:], [T_l2s], [T_lb2]),
                    lambda: TTo("dve", e1[:], o1s[:], lb1[:], ALU.mult, [T_o1s, T_lb1], [T_e1]),
                    lambda: TTo("dve", e2[:], o2s[:], lb2[:], ALU.mult, [T_o2s, T_lb2], [T_e2]),
                    lambda: STT("dve", osb[:], e2[:], neglam[:, 0:1], e1[:], ALU.mult, ALU.add, [T_e1, T_e2, T_const], [T_osb]),
                    lambda: TTo("dve", sqo[:], osb[:], osb[:], ALU.mult, [T_osb], [T_sqo]),
                    None, None,
                    lambda: MM(ps[:, 6, :], ones_bf[:], sqo[:], True, True, [T_sqo, T_const], [PB[6]]),
                    None, None,
                    lambda: ACT(rso[:], ps[:, 6, :], AF.Ln, [PB[6]], [T_rso], scale=1.0 / 128, bias=EPS),
                    None,
                    lambda: ACT(rso[:], rso[:], AF.Exp, [T_rso], [T_rso], scale=-0.5),
                    None,
                    lambda: STT("dve", oast[ob][:], osb[:], gsub8[:, 0:1], rso[:], ALU.mult, ALU.mult,
                                [T_osb, T_rso, T_const], [T_oast[ob]]),
                    lambda: DMA("pool", oa_s[h, :, qsl], oast[ob][:], f"oa_w{ob}", [T_oast[ob]], [T_oa]),
                ]
            epi_q.extend(mk())
    while epi_q:
        epi_pop()

    P.barrier()
    A.off = setup_mark
    xtok = A.alloc([128, 4, 1024], F32)
    xT = A.alloc([128, 8, TT], F32)
    hT = A.alloc([128, 8, TT], BF16)
    sq = A.alloc([128, 8, TT], BF16)
    aT = A.alloc([128, NFC, TT], BF16)
    yT = A.alloc([128, 8, TT], F32)
    rs = A.alloc([128, TT], F32)
    tmpa = [A.alloc([128, TT], F32) for _ in range(2)]
    sg = [A.alloc([128, TT], F32) for _ in range(2)]
    wpool = [A.alloc([128, 4096], BF16) for _ in range(NW)]
    oaTb = [A.alloc([128, 8, TT], BF16) for _ in range(2)]
    odTb = [A.alloc([128, 2, TT], BF16) for _ in range(2)]
    h2Lb = [A.alloc([128, 8, TT], BF16) for _ in range(2)]
    T_oaTb, T_odTb, T_h2Lb = [Tok(), Tok()], [Tok(), Tok()], [Tok(), Tok()]
    mg = A.alloc([128, 8, TT], BF16)
    wpbt = A.alloc([128, 2048], BF16)
    twpb = Tok()
    DMA("sp", wpbt[:], w_s["wpb"][0], "wpb_l", w_blk_all["wpb"], [twpb])
    outs = []
    tmpc = [A.alloc([128, TT], F32) for _ in range(2)]
    T_tmpc = [Tok(), Tok()]
    NCT = cfg["C_tiles"]
    wo_pref = [None]

    def c_loads(tt):
        b_ = tt % 2
        sl_ = slice(tt * TT, (tt + 1) * TT)
        DMA("sp", oaTb[b_][:], oa_s[:, :, sl_].rearrange("h p n -> p h n"), f"oal{b_}", [T_oa], [T_oaTb[b_]])
        DMA("sp", odTb[b_][:], od_s[:, :, sl_].rearrange("c p n -> p c n"), f"odl{b_}", [T_od], [T_odTb[b_]])
        DMA("sp", h2Lb[b_][:].rearrange("p c n -> p (c n)"), h2_s[tt], f"h2l{b_}", [T_h2], [T_h2Lb[b_]])

    def c_proj(tt, slot):
        oaT, odT, h2L = oaTb[tt % 2], odTb[tt % 2], h2Lb[tt % 2]
        T_oaT, T_odT, T_h2L = T_oaTb[tt % 2], T_odTb[tt % 2], T_h2Lb[tt % 2]
        for o in range(8):
            bset = (2, 3, 4, 5) if o % 2 == 0 else (6, 7, 0, 5)
            if o % 2 == 0:
                wpa_b = wload("wpa", o // 2, 2048)
                wga_b = wload("wg", o // 2, 2048)
                wgb_b = wload("wg", 4 + o // 2, 2048)
            co = (o % 2) * 128
            for kc in range(8):
                MM(ps[:, bset[0], :], wpa_b[0][:, kc * 256 + co:kc * 256 + co + 128], oaT[:, kc, :], kc == 0, kc == 7,
                   [wpa_b[1], T_oaT], [PB[bset[0]]])
            for kc in range(8):
                MM(ps[:, bset[1], :], wga_b[0][:, kc * 256 + co:kc * 256 + co + 128], h2L[:, kc, :], kc == 0, kc == 7,
                   [wga_b[1], T_h2L], [PB[bset[1]]])
            for kc in range(2):
                MM(ps[:, bset[2], :], wpbt[:, kc * 1024 + o * 128:kc * 1024 + (o + 1) * 128], odT[:, kc, :], kc == 0, kc == 1,
                   [twpb, T_odT], [PB[bset[2]]])
            for kc in range(8):
                MM(ps[:, bset[3], :], wgb_b[0][:, kc * 256 + co:kc * 256 + co + 128], h2L[:, kc, :], kc == 0, kc == 7,
                   [wgb_b[1], T_h2L], [PB[bset[3]]])
            ACT(sg[0][:], ps[:, bset[1], :], AF.Sigmoid, [PB[bset[1]]], [T_sg[0]])
            TTo("dve", tmpa[0][:], sg[0][:], ps[:, bset[0], :], ALU.mult, [T_sg[0], PB[bset[0]]], [T_tmpa[0]])
            ACT(sg[1][:], ps[:, bset[3], :], AF.Sigmoid, [PB[bset[3]]], [T_sg[1]])
            TTo("dve", tmpa[1][:], sg[1][:], ps[:, bset[2], :], ALU.mult, [T_sg[1], PB[bset[2]]], [T_tmpa[1]])
            TTo("pool", mg[:, o, :], tmpa[0][:], tmpa[1][:], ALU.add, [T_tmpa[0], T_tmpa[1]], [T_mg])
            slot()

    def c_chain1():
        st = Stats()
        q = []
        for c in (6, 7):
            q.append(lambda c=c: TTo("pool", yT[:, c, :], yT[:, c, :], rs[:], ALU.mult, [T_yc[c], T_rs], [T_yc[c]]))
            q.append(lambda c=c: TTo("pool", xT[:, c, :], xT[:, c, :], yT[:, c, :], ALU.add, [T_xc[c], T_yc[c]], [T_xc[c]]))
        for c in range(6):
            k = c % 2
            q.append(lambda c=c, k=k: TTo("dve", tmpc[k][:], yT[:, c, :], rs[:], ALU.mult, [T_yc[c], T_rs], [T_tmpc[k]]))
            q.append(lambda c=c, k=k: TTo("dve", xT[:, c, :], xT[:, c, :], tmpc[k][:], ALU.add, [T_xc[c], T_tmpc[k]], [T_xc[c]]))
            q.append(lambda c=c: ACT(sq[:, c, :], xT[:, c, :], AF.Square, [T_xc[c]], [T_sqc[c]]))
            q.append(lambda c=c: st.push(c))
        for c in (6, 7):
            q.append(lambda c=c: ACT(sq[:, c, :], xT[:, c, :], AF.Square, [T_xc[c]], [T_sqc[c]]))
            q.append(lambda c=c: st.push(c))
        q.append(lambda: st.finish(D))
        for c in range(8):
            q.append(lambda c=c: STT("dve", hT[:, c, :], xT[:, c, :], gcols[:, 32 + c:33 + c], rs[:], ALU.mult, ALU.mult,
                                     [T_xc[c], T_rs, T_const], [T_hc[c]]))
        return q

    if NCT > 0:
        c_loads(0)
        if NCT > 1:
            c_loads(1)
        c_proj(0, lambda: None)
    for t in range(NCT):
        tsl = slice(t * TT, (t + 1) * TT)
        if t + 2 < NCT:
            c_loads(t + 2)
        sto = Stats()
        if wo_pref[0] is None:
            wo_pref[0] = [wload("wo", b_, 2048) for b_ in range(4)]
        wo_blocks = wo_pref[0]
        wo_pref[0] = None
        for o in range(8):
            if o % 2 == 0:
                wo_b = wo_blocks[o // 2]
            if o == 6:
                DMA("sp", xT[:].rearrange("p c n -> p (c n)"), x1_s[t], "x1l", [T_x1, T_oa, T_od], T_xc)
            co = (o % 2) * 128
            bank = 6 + o % 2
            for kc in range(8):
                MM(ps[:, bank, :], wo_b[0][:, kc * 256 + co:kc * 256 + co + 128], mg[:, kc, :], kc == 0, kc == 7,
                   [wo_b[1], T_mg], [PB[bank]])
            evac_y(o, bank, gcols, 24, sto)
        sto.finish(D)
        q = c_chain1()
        if t + 1 < NCT:
            def slot():
                for _ in range(6):
                    if q:
                        q.pop(0)()
            c_proj(t + 1, slot)
        while q:
            q.pop(0)()
        ffn("wgu2", "wd2", ghalf, 8)
        resid_apply(None)
        if t + 1 < NCT:
            wo_pref[0] = [wload("wo", b_, 2048) for b_ in range(4)]
        cnt = 0
        for j in range(4):
            for c4 in range(2):
                bank = 6 + cnt % 2
                cnt += 1
                for cq in range(4):
                    c = c4 * 4 + cq
                    TR(ps[:, bank, cq * 128:(cq + 1) * 128], xT[:, c, j * 128:(j + 1) * 128], [T_xc[c]], [PB[bank]])
                CP("dve", xtok[:, j, c4 * 512:(c4 + 1) * 512], ps[:, bank, :], [PB[bank]], [T_xtok])
        outs.append(DMA("sp", out_ap[tsl, :].rearrange("(j p) d -> p j d", p=128), xtok[:], "out_w", [T_xtok], []))

    if dbg:
        P.barrier()
        dbg_o = nc.dram_tensor("dbg_o", [128, 128], F32, kind="ExternalOutput").ap()
        outs.append(DMA("sp", dbg_o, ident[:], "dbg_w", [T_const], []))
    P.emit(outs)
    return nc


def _blocks_km(W, cb):
    K, N = W.shape
    r = W.reshape(K // 128, 128, N // cb, cb).transpose(2, 1, 0, 3)
    return np.ascontiguousarray(r.reshape(N // cb, 128, (K // 128) * cb))


_NC_CACHE = {}


def prep_inputs(x, positions, w_in, lambda_q1, lambda_k1, lambda_q2, lambda_k2, g_subln,
                w_proj_a, w_proj_b, w_out, w_gu1, w_down1, w_gu2, w_down2,
                g_pre_ffn1, g_post_ffn1, g_pre_mix, g_post_mix, g_pre_ffn2, g_post_ffn2):
    f32 = np.float32
    x = np.asarray(x, f32)
    positions = np.asarray(positions, np.int32)
    w_in = np.asarray(w_in, f32)[0]

    def gu_blocks(W):
        W = np.asarray(W, f32)[0]
        r = W.reshape(8, 128, 2, 11, 256).transpose(3, 1, 0, 2, 4)
        return np.ascontiguousarray(r.reshape(11, 128, 4096))

    def down_blocks(W):
        W = np.asarray(W, f32)[0]
        r = W.reshape(22, 128, 8, 128).transpose(2, 1, 0, 3)
        return np.ascontiguousarray(r.reshape(8, 128, 2816))

    col_ka, col_kd, col_qa, col_qd = 1024, 3072 + 768, 0, 3072
    chunk_cols = ([col_ka + 128 * i for i in range(8)] + [col_kd + 128 * i for i in range(6)]
                  + [col_qa + 128 * i for i in range(8)] + [col_qd + 128 * i for i in range(6)])
    partner = np.arange(128)
    for hb in (0, 64):
        partner[hb:hb + 8] = np.arange(hb + 8, hb + 16)
        partner[hb + 8:hb + 16] = np.arange(hb, hb + 8)
    wrope = np.empty((28, 128, 8, 2, 128), f32)
    for i, c0 in enumerate(chunk_cols):
        Wc = w_in[:, c0:c0 + 128].reshape(8, 128, 128).transpose(1, 0, 2)
        wrope[i, :, :, 0, :] = Wc
        wrope[i, :, :, 1, :] = Wc[:, :, partner]
    wrope = wrope.reshape(28, 128, 2048)
    wv = _blocks_km(np.concatenate([w_in[:, 2048:3072], w_in[:, 3072 + 1536:3072 + 2304]], axis=1), 256)
    wg = _blocks_km(w_in[:, 5376:7424], 256)
    wpa = _blocks_km(np.asarray(w_proj_a, f32)[0], 256)
    wo = _blocks_km(np.asarray(w_out, f32)[0], 256)
    wpb = _blocks_km(np.asarray(w_proj_b, f32)[0], 1024)
    weights = {"wgu1": gu_blocks(w_gu1), "wd1": down_blocks(w_down1), "wrope": wrope, "wv": wv, "wg": wg,
               "wpa": wpa, "wpb": wpb, "wo": wo, "wgu2": gu_blocks(w_gu2), "wd2": down_blocks(w_down2)}

    gl = [g_pre_ffn1, g_post_ffn1, g_pre_mix, g_post_mix, g_pre_ffn2, g_post_ffn2]
    gcols = np.concatenate([np.asarray(g, f32)[0].reshape(8, 128).T for g in gl], axis=1)
    gcols = np.ascontiguousarray(gcols)
    gsub = np.ascontiguousarray(np.asarray(g_subln, f32)[0].reshape(128, 1))
    lam4 = np.concatenate([np.asarray(v, f32)[0] for v in (lambda_q1, lambda_k1, lambda_q2, lambda_k2)])
    lam4 = np.ascontiguousarray(np.broadcast_to(lam4[None, :], (128, 256)))
    ident = np.eye(128, dtype=f32)
    pm = np.arange(128) % 64
    inv = (THETA ** (-(np.arange(128) % 8) / 8.0)).astype(f32)
    sgn = np.where(pm < 8, -1.0, np.where(pm < 16, 1.0, 0.0)).astype(f32)
    invsgn = np.ascontiguousarray(np.stack([inv, sgn], axis=1))
    kk = np.arange(128)[:, None]
    qq = np.arange(128)[None, :]
    lo = (kk >= qq).astype(f32)
    hi = (kk <= qq).astype(f32)

    def mask_tile(lo_m, hi_m):
        return np.concatenate([lo_m, lo_m, hi_m, hi_m, lo_m, lo_m, hi_m, hi_m], axis=1)

    in_maps = []
    for c in range(8):
        b, half = c // 2, c % 2
        T0 = half * SO
        lo_first = lo.copy()
        hi_last = hi.copy()
        if half == 0:
            lo_first[0:64, :] = 0.0
        else:
            hi_last[64:128, :] = 0.0
        masks = np.stack([mask_tile(lo, hi), mask_tile(lo_first, hi), mask_tile(lo, hi_last)], axis=1)
        m = {
            "x_loc": np.ascontiguousarray(np.roll(x[b], -T0, axis=0)),
            "pos_gf": np.ascontiguousarray(np.repeat(np.roll(positions[b], -T0).reshape(16, 1, 512), 8, axis=1).reshape(128, 512)),
            "ident": ident, "invsgn": invsgn, "masks": np.ascontiguousarray(masks.reshape(128, 3072)),
            "gcols": gcols, "gsub": gsub, "lam4": lam4,
        }
        m.update(weights)
        in_maps.append(m)
    return in_maps


def kernel(**inputs):
    f32 = np.float32
    in_maps = prep_inputs(**inputs)
    if "nc" not in _NC_CACHE:
        _NC_CACHE["nc"] = build_program()
    nc = _NC_CACHE["nc"]
    res = run_bass_kernel_spmd(nc, in_maps, core_ids=list(range(8)))
    out = np.empty((4, S, D), f32)
    for c in range(8):
        b, half = c // 2, c % 2
        out[b, half * SO:(half + 1) * SO, :] = res.results[c]["out"]
    return out
```

```python
import math
import numpy as np
import concourse.bass as bass
import concourse.mybir as mybir
from concourse.bass_utils import run_bass_kernel_spmd

F32 = mybir.dt.float32
BF16 = mybir.dt.bfloat16
I32 = mybir.dt.int32
AF = mybir.ActivationFunctionType
ALU = mybir.AluOpType
AX = mybir.AxisListType

S = 8192
D = 1024
SO = 4096
TT = 512
NFC = 22
EPS = 1e-6
THETA = 500000.0
SB_BASE = 16512
SB_END = 229376


class Tok:
    __slots__ = ("w", "wd", "r", "rd")

    def __init__(self):
        self.w = {}
        self.wd = []
        self.r = {}
        self.rd = []


class Op:
    __slots__ = ("eng", "fn", "deps", "sig", "sem", "val", "dma")

    def __init__(self, eng, fn, deps, dma):
        self.eng, self.fn, self.deps, self.dma = eng, fn, deps, dma
        self.sig = False
        self.sem = None
        self.val = 0


class Prog:
    ENGS = ("pe", "act", "dve", "pool", "sp")
    SEM_LIMIT = 30000

    def __init__(self, nc):
        self.nc = nc
        self.ops = {e: [] for e in self.ENGS}
        self.all = []
        self.nsem = 0
        self.last = {}
        self.dmas = []
        self.pending = {e: [] for e in self.ENGS}

    def barrier(self):
        allp = list(self.last.values()) + self.dmas
        self.dmas = []
        for e in self.ENGS:
            self.pending[e] = list(allp)

    def add(self, eng, fn, r=(), w=(), dma=None, deps=(), nobar=False, nowaw=False):
        dl = [d for d in deps if d is not None]
        if self.pending[eng]:
            dl.extend(self.pending[eng])
            self.pending[eng] = []
        for t in r:
            dl.extend(t.w.values())
            dl.extend(t.wd)
        for t in w:
            if not nowaw:
                dl.extend(t.w.values())
                dl.extend(t.wd)
            dl.extend(t.r.values())
            dl.extend(t.rd)
        if eng == "pe":
            dl = [d for d in dl if d.dma is not None or d.eng != "pe"]
        elif dma is None:
            n = len(self.ops[eng])
            recent = set(id(o) for o in self.ops[eng][max(0, n - 2):])
            dl = [d for d in dl if d.dma is not None or d.eng != eng or id(d) in recent]
        op = Op(eng, fn, dl, dma)
        for d in dl:
            d.sig = True
        for t in w:
            if t.r or t.rd:
                t.w, t.wd, t.r, t.rd = {}, [], {}, []
            if dma is None:
                t.w[eng] = op
            else:
                t.wd.append(op)
        for t in r:
            if dma is None:
                t.r[eng] = op
            else:
                t.rd.append(op)
        self.ops[eng].append(op)
        self.all.append(op)
        if dma is None:
            self.last[eng] = op
        elif not nobar:
            self.dmas.append(op)
        return op

    def finalize(self):
        cur = {}
        for op in self.all:
            if op.dma is not None:
                key, inc, need = ("dma", op.dma), 16, True
            else:
                key, inc, need = ("eng", op.eng), 1, op.sig
            if not need:
                continue
            if key not in cur or cur[key][1] + inc > self.SEM_LIMIT:
                self.nsem += 1
                cur[key] = [self.nc.alloc_semaphore(f"s{self.nsem}"), 0]
            cur[key][1] += inc
            op.sem, op.val = cur[key]

    def replay(self, eng, e):
        waited = {}
        for op in self.ops[eng]:
            for d in op.deps:
                k = id(d.sem)
                if waited.get(k, 0) < d.val:
                    e.wait_ge(d.sem, d.val)
                    waited[k] = d.val
            ins = op.fn(e)
            if op.sem is not None:
                ins.then_inc(op.sem, 16 if op.dma is not None else 1)

    def emit(self, final_ops):
        for d in final_ops:
            d.sig = True
        self.finalize()
        with self.nc.Block() as block:
            @block.tensor
            def _(e):
                self.replay("pe", e)

            @block.scalar
            def _(e):
                self.replay("act", e)

            @block.vector
            def _(e):
                self.replay("dve", e)

            @block.gpsimd
            def _(e):
                self.replay("pool", e)

            @block.sync
            def _(e):
                self.replay("sp", e)
                done = {}
                for d in final_ops:
                    if done.get(id(d.sem), 0) < d.val:
                        done[id(d.sem)] = d.val
                        e.wait_ge(d.sem, d.val)


class Arena:
    def __init__(self, nc):
        self.nc = nc
        self.off = SB_BASE
        self.n = 0

    def alloc(self, shape, dtype):
        esz = 2 if dtype == BF16 else 4
        nbytes = int(np.prod(shape[1:])) * esz
        self.off = (self.off + 63) // 64 * 64
        self.n += 1
        t = self.nc.alloc_sbuf_tensor_at(f"t{self.n}", list(shape), dtype, offset=self.off)
        self.off += nbytes
        assert self.off <= SB_END, f"SBUF overflow {self.off}"
        return t


def build_program(cfg=None):
    base = dict(B2m=5, B2g=(0, 1, 2), B2c=True, A_tiles=16, B2=True, B1_heads=8, B1_qc=8, C_tiles=8, dbg=False)
    base.update(cfg or {})
    cfg = base
    dbg = cfg["dbg"]
    nc = bass.Bass("TRN2", target_bir_lowering=False)
    P = Prog(nc)

    def din(name, shape, dt=F32):
        return nc.dram_tensor(name, list(shape), dt, kind="ExternalInput").ap()

    def dscr(name, shape, dt=BF16):
        kind = "ExternalOutput" if (dbg and not name.startswith("w")) else "Internal"
        return nc.dram_tensor(name, list(shape), dt, kind=kind).ap()

    x_in = din("x_loc", [S, D])
    pos_in = din("pos_gf", [128, 512], I32)
    ident_in = din("ident", [128, 128])
    invsgn_in = din("invsgn", [128, 2])
    masks_in = din("masks", [128, 3 * 1024])
    gcols_in = din("gcols", [128, 48])
    gsub_in = din("gsub", [128, 1])
    lam_in = din("lam4", [128, 256])
    WSPEC = [("wgu1", 11, 4096), ("wd1", 8, 2816), ("wrope", 28, 2048), ("wv", 7, 2048),
             ("wg", 8, 2048), ("wpa", 4, 2048), ("wpb", 1, 2048), ("wo", 4, 2048),
             ("wgu2", 11, 4096), ("wd2", 8, 2816)]
    w_in_ap, w_s, w_tok = {}, {}, {}
    for nm, nb, xx in WSPEC:
        w_in_ap[nm] = din(nm, [nb, 128, xx])
        w_s[nm] = dscr(nm + "_s", [nb, 128, xx])
        w_tok[nm] = Tok()
    out_ap = nc.dram_tensor("out", [SO, D], F32, kind="ExternalOutput").ap()

    cs_c = dscr("cs_c", [3, 16, 8, 512], F32)
    kT_s = dscr("kT_s", [14, 128, S])
    qT_s = dscr("qT_s", [14, 128, SO])
    v_s = dscr("v_s", [S, 1792])
    x1_s = dscr("x1_s", [8, 128, 8 * TT], F32)
    h2_s = dscr("h2_s", [8, 128, 8 * TT])
    oa_s = dscr("oa_s", [8, 128, SO])
    od_s = dscr("od_s", [2, 128, SO])
    T_cs, T_kT, T_qT, T_v, T_x1, T_h2, T_oa, T_od = (Tok() for _ in range(8))

    ps = nc.alloc_psum_tensor("ps", [128, 8, 512], F32)
    PB = [Tok() for _ in range(8)]

    A = Arena(nc)
    ident = A.alloc([128, 128], F32)
    ones_bf = A.alloc([128, 128], BF16)
    ones_f = A.alloc([128, 128], F32)
    masks = A.alloc([128, 3, 1024], F32)
    gcols = A.alloc([128, 48], F32)
    ghalf = A.alloc([128, 16], F32)
    gsub8 = A.alloc([128, 1], F32)
    neglam = A.alloc([128, 1], F32)
    invsgn = A.alloc([128, 2], F32)
    T_const = Tok()
    const_mark = A.off

    def MM(out, lhsT, rhs, start, stop, r, w, tp=None):
        if tp is None:
            return P.add("pe", lambda e: e.matmul(out, lhsT=lhsT, rhs=rhs, start=start, stop=stop), r=r, w=w)
        return P.add("pe", lambda e: e.matmul(out, lhsT=lhsT, rhs=rhs, start=start, stop=stop,
                                              tile_position=tp), r=r, w=w)

    def TR(out, in_, r, w):
        return P.add("pe", lambda e: e.transpose(out, in_, ident[:]), r=list(r) + [T_const], w=w)

    def ACT(out, in_, func, r, w, scale=1.0, bias=0.0):
        return P.add("act", lambda e: e.activation(out=out, in_=in_, func=func, bias=bias, scale=scale), r=r, w=w)

    def CP(eng, out, in_, r, w):
        return P.add(eng, lambda e: e.tensor_copy(out, in_), r=r, w=w)

    def TTo(eng, out, in0, in1, op, r, w):
        return P.add(eng, lambda e: e.tensor_tensor(out=out, in0=in0, in1=in1, op=op), r=r, w=w)

    def TS(eng, out, in0, s1, s2, op0, op1, r, w):
        if s2 is None:
            return P.add(eng, lambda e: e.tensor_scalar(out=out, in0=in0, scalar1=s1, scalar2=None, op0=op0), r=r, w=w)
        return P.add(eng, lambda e: e.tensor_scalar(out=out, in0=in0, scalar1=s1, scalar2=s2, op0=op0, op1=op1), r=r, w=w)

    def STT(eng, out, in0, sc, in1, op0, op1, r, w):
        return P.add(eng, lambda e: e.scalar_tensor_tensor(out=out, in0=in0, scalar=sc, in1=in1, op0=op0, op1=op1),
                     r=r, w=w)

    def RCP(out, in_, r, w):
        return P.add("dve", lambda e: e.reciprocal(out, in_), r=r, w=w)

    def DMA(eng, out, in_, key, r, w, nobar=False, nowaw=False, deps=()):
        return P.add(eng, lambda e: e.dma_start(out=out, in_=in_), r=r, w=w, dma=key, nobar=nobar, nowaw=nowaw, deps=deps)

    DMA("sp", ident[:], ident_in, "c0", [], [T_const], nowaw=True)
    DMA("sp", masks[:].rearrange("p a b -> p (a b)"), masks_in, "c0", [], [T_const], nowaw=True)
    DMA("sp", gcols[:], gcols_in, "c0", [], [T_const], nowaw=True)
    DMA("sp", invsgn[:], invsgn_in, "c0", [], [T_const], nowaw=True)
    P.add("dve", lambda e: e.memset(ones_bf[:], 1.0), w=[T_const])
    P.add("dve", lambda e: e.memset(ones_f[:], 1.0), w=[T_const])
    w_btok = {}
    w_blk_all = {}
    conv_later = []

    def conv_block(nm, b, deps=()):
        if nm in ("wgu1", "wd1"):
            DMA("pool", w_s[nm][b], w_in_ap[nm][b], f"cv_{nm}_{b}", [], [w_btok[(nm, b)]], nobar=True, deps=deps)
        else:
            DMA("pool", w_s[nm][b], w_in_ap[nm][b], "cv_" + nm, [], [w_blk_all[nm][b]], nobar=True, deps=deps)

    for nm, nb, xx in WSPEC:
        for b in range(nb):
            if nm in ("wgu1", "wd1"):
                w_btok[(nm, b)] = Tok()
            else:
                w_blk_all.setdefault(nm, []).append(Tok())
    T_xtok0 = Tok()
    x0_load = [None]

    setup_mark = A.off
    T_tmp = Tok()
    gs_t = A.alloc([128, 1], F32)
    lam_t = A.alloc([128, 256], F32)
    lpr = A.alloc([128, 128], F32)
    ld = A.alloc([128, 2], F32)
    le = A.alloc([128, 2], F32)
    DMA("sp", gs_t[:], gsub_in, "c1", [], [T_tmp], nowaw=True)
    DMA("sp", lam_t[:], lam_in, "c1", [], [T_tmp], nowaw=True)
    TS("dve", ghalf[:, 0:8], gcols[:, 8:16], 0.5, None, ALU.mult, None, [T_const], [T_const])
    TS("dve", ghalf[:, 8:16], gcols[:, 40:48], 0.5, None, ALU.mult, None, [T_const], [T_const])
    TS("dve", gsub8[:], gs_t[:], 0.8, None, ALU.mult, None, [T_tmp], [T_const])
    T_l = Tok()
    TTo("dve", lpr[:, 0:64], lam_t[:, 0:64], lam_t[:, 64:128], ALU.mult, [T_tmp], [T_l])
    TTo("dve", lpr[:, 64:128], lam_t[:, 128:192], lam_t[:, 192:256], ALU.mult, [T_tmp], [T_l])
    P.add("dve", lambda e: e.tensor_reduce(out=ld[:, 0:1], in_=lpr[:, 0:64], axis=AX.X, op=ALU.add), r=[T_l], w=[T_l])
    P.add("dve", lambda e: e.tensor_reduce(out=ld[:, 1:2], in_=lpr[:, 64:128], axis=AX.X, op=ALU.add), r=[T_l], w=[T_l])
    ACT(le[:], ld[:], AF.Exp, [T_l], [T_l])
    TTo("dve", ld[:, 0:1], le[:, 1:2], le[:, 0:1], ALU.subtract, [T_l], [T_l])
    TS("dve", neglam[:], ld[:, 0:1], -0.2, None, ALU.add, None, [T_l], [T_const])

    RC = 512
    posi = A.alloc([128, RC], I32)
    posf = A.alloc([128, RC], F32)
    ang = A.alloc([128, RC], F32)
    kf = A.alloc([128, RC], F32)
    ki = A.alloc([128, RC], I32)
    fx = A.alloc([128, RC], F32)
    kn = A.alloc([128, RC], F32)
    T_pos, T_ang, T_k, T_kn = Tok(), Tok(), Tok(), Tok()
    TWO_PI = 2 * math.pi
    cs_flat = cs_c.rearrange("k g f n -> k (g f) n")

    def reduce_sin(shift, kidx, neg_idx):
        TS("dve", fx[:], ang[:], shift, None, ALU.add, None, [T_ang], [T_k])
        TS("dve", kf[:], fx[:], 1.0 / TWO_PI, 0.5, ALU.mult, ALU.add, [T_k], [T_k])
        CP("dve", ki[:], kf[:], [T_k], [T_k])
        CP("dve", kf[:], ki[:], [T_k], [T_k])
        STT("dve", fx[:], kf[:], -TWO_PI, fx[:], ALU.mult, ALU.add, [T_k], [T_k])
        TS("dve", kf[:], fx[:], math.pi, -TWO_PI, ALU.is_gt, ALU.mult, [T_k], [T_k])
        TTo("dve", fx[:], fx[:], kf[:], ALU.add, [T_k], [T_k])
        TS("dve", kf[:], fx[:], -math.pi, TWO_PI, ALU.is_lt, ALU.mult, [T_k], [T_k])
        TTo("dve", fx[:], fx[:], kf[:], ALU.add, [T_k], [T_k])
        TS("dve", fx[:], fx[:], -math.pi, math.pi, ALU.max, ALU.min, [T_k], [T_k])
        ACT(kf[:], fx[:], AF.Sin, [T_k], [T_k])
        DMA("sp", cs_flat[kidx], kf[:], "cs_w", [T_k], [T_cs])
        if neg_idx is not None:
            TS("dve", kn[:], kf[:], -1.0, None, ALU.mult, None, [T_k], [T_kn])
            DMA("sp", cs_flat[neg_idx], kn[:], "cs_w", [T_kn], [T_cs])

    DMA("sp", posi[:], pos_in, "pos", [], [T_pos])
    CP("dve", posf[:], posi[:], [T_pos], [T_pos])
    TS("dve", ang[:], posf[:], invsgn[:, 0:1], None, ALU.mult, None, [T_pos, T_const], [T_ang])
    reduce_sin(math.pi / 2, 0, None)
    reduce_sin(0.0, 2, 1)

    P.barrier()
    A.off = setup_mark
    xtok = A.alloc([128, 4, 1024], F32)
    xT = A.alloc([128, 8, TT], F32)
    hT = A.alloc([128, 8, TT], BF16)
    sq = A.alloc([128, 8, TT], BF16)
    aT = A.alloc([128, NFC, TT], BF16)
    yT = A.alloc([128, 8, TT], F32)
    rs = A.alloc([128, TT], F32)
    tmpa = [A.alloc([128, TT], F32) for _ in range(2)]
    sg = [A.alloc([128, TT], F32) for _ in range(2)]
    cs2 = [A.alloc([128, 2, TT], F32) for _ in range(2)]
    T_cs2 = [Tok(), Tok()]
    kst = A.alloc([128, 14, TT], BF16)
    vst = A.alloc([128, 4, 1792], BF16)
    NW = 5
    wpool = [A.alloc([128, 4096], BF16) for _ in range(NW)]
    h2T = [A.alloc([128, 8, TT], BF16) for _ in range(2)]
    T_h2c = [[Tok() for _ in range(8)] for _ in range(2)]
    tmpb = [A.alloc([128, TT], F32) for _ in range(2)]
    T_tmpb = [Tok(), Tok()]
    T_xtok, T_xT, T_hT, T_sq, T_yT, T_rs, T_cst, T_kst, T_vst, T_oaT, T_odT, T_mg = (Tok() for _ in range(12))
    T_aT = [Tok() for _ in range(NFC)]
    T_tmpa = [Tok(), Tok()]
    T_sg = [Tok(), Tok()]
    T_w = [Tok() for _ in range(NW)]
    wctr = [0]
    tctr = [0]

    def wload(nm, b, nelem):
        i = wctr[0] % NW
        wctr[0] += 1
        rd = [w_btok[(nm, b)]] if (nm, b) in w_btok else w_blk_all[nm]
        DMA("sp", wpool[i][:, 0:nelem], w_s[nm][b][:, 0:nelem], f"w{i}", rd, [T_w[i]])
        return wpool[i], T_w[i]

    T_sqc = [Tok() for _ in range(8)]
    T_yc = [Tok() for _ in range(8)]
    T_xc = [Tok() for _ in range(8)]
    T_hc = [Tok() for _ in range(8)]

    T_ser = Tok()

    class Stats:
        def __init__(self):
            self.pending = None
            self.n = 0

        def push(self, c):
            self.flush()
            self.pending = c

        def flush(self):
            if self.pending is not None:
                c = self.pending
                MM(ps[:, 1, :], ones_bf[:], sq[:, c, :], self.n == 0, self.n == 7, [T_sqc[c], T_const], [PB[1]])
                self.n += 1
                self.pending = None

        def finish(self, n_feat):
            self.flush()
            assert self.n == 8
            ACT(rs[:], ps[:, 1, :], AF.Ln, [PB[1]], [T_rs], scale=1.0 / n_feat, bias=EPS)
            ACT(rs[:], rs[:], AF.Exp, [T_rs], [T_rs], scale=-0.5)

    def h_from_x(gbase):
        for c in range(8):
            eng = "dve"
            STT(eng, hT[:, c, :], xT[:, c, :], gcols[:, gbase + c:gbase + c + 1], rs[:], ALU.mult, ALU.mult,
                [T_xc[c], T_rs, T_const], [T_hc[c]])

    def resid_apply(next_gbase):
        st = Stats()
        for c in (6, 7):
            TTo("pool", yT[:, c, :], yT[:, c, :], rs[:], ALU.mult, [T_yc[c], T_rs], [T_yc[c]])
            TTo("pool", xT[:, c, :], xT[:, c, :], yT[:, c, :], ALU.add, [T_xc[c], T_yc[c]], [T_xc[c]])
        for c in range(6):
            k = tctr[0] % 2
            tctr[0] += 1
            TTo("dve", tmpa[k][:], yT[:, c, :], rs[:], ALU.mult, [T_yc[c], T_rs], [T_tmpa[k]])
            TTo("dve", xT[:, c, :], xT[:, c, :], tmpa[k][:], ALU.add, [T_xc[c], T_tmpa[k]], [T_xc[c]])
            if next_gbase is not None:
                ACT(sq[:, c, :], xT[:, c, :], AF.Square, [T_xc[c]], [T_sqc[c]])
                st.push(c)
        for c in (6, 7):
            if next_gbase is not None:
                ACT(sq[:, c, :], xT[:, c, :], AF.Square, [T_xc[c]], [T_sqc[c]])
                st.push(c)
        if next_gbase is not None:
            st.finish(D)
            h_from_x(next_gbase)

    def evac_y(o, bank, gtile, gbase, st):
        ACT(sq[:, o, :], ps[:, bank, :], AF.Square, [PB[bank]], [T_sqc[o], T_ser])
        st.push(o)
        TS("dve", yT[:, o, :], ps[:, bank, :], gtile[:, gbase + o:gbase + o + 1], None, ALU.mult, None,
           [PB[bank], T_const, T_ser], [T_yc[o]])

    def ffn(wgu, wd, gtile, gbase, mid=None):
        for blk in range(11):
            if blk == 6 and mid is not None:
                mid()
            wb, tw = wload(wgu, blk, 4096)
            for fl in range(2):
                f = blk * 2 + fl
                st = f % 2
                bg, bu = 2 + 2 * st, 3 + 2 * st
                for gu, bank in ((0, bg), (1, bu)):
                    for kc in range(8):
                        o0 = kc * 512 + gu * 256 + fl * 128
                        MM(ps[:, bank, :], wb[:, o0:o0 + 128], hT[:, kc, :], kc == 0, kc == 7, [tw, T_hc[kc]], [PB[bank]])
                ACT(sg[st][:], ps[:, bg, :], AF.Silu, [PB[bg]], [T_sg[st]])
                TTo("dve", aT[:, f, :], sg[st][:], ps[:, bu, :], ALU.mult, [T_sg[st], PB[bu]], [T_aT[f]])
        stt = Stats()
        for o in range(8):
            wb, tw = wload(wd, o, 2816)
            bank = 6 + o % 2
            for f in range(NFC):
                MM(ps[:, bank, :], wb[:, f * 128:(f + 1) * 128], aT[:, f, :], f == 0, f == NFC - 1,
                   [tw, T_aT[f]], [PB[bank]])
            evac_y(o, bank, gtile, gbase, stt)
        stt.finish(D)

    def rope_proj(blocks, dst_scr, T_dst, t, hsrc, T_hs, csb, T_csb, slot):
        for i, b in enumerate(blocks):
            wb, tw = wload("wrope", b, 2048)
            st = i % 2
            bp, br = 2 + 2 * st, 3 + 2 * st
            for rot, bank in ((0, bp), (1, br)):
                for kc in range(8):
                    o0 = kc * 256 + rot * 128
                    MM(ps[:, bank, :], wb[:, o0:o0 + 128], hsrc[:, kc, :], kc == 0, kc == 7, [tw, T_hs[kc]], [PB[bank]])
            TTo("dve", tmpb[0][:], ps[:, bp, :], csb[:, 0, :], ALU.mult, [PB[bp], T_csb], [T_tmpb[0]])
            TTo("dve", tmpb[1][:], ps[:, br, :], csb[:, 1, :], ALU.mult, [PB[br], T_csb], [T_tmpb[1]])
            TTo("pool", kst[:, i, :], tmpb[0][:], tmpb[1][:], ALU.add, [T_tmpb[0], T_tmpb[1]], [T_kst])
            slot()
        DMA("pool", dst_scr[:, :, t * TT:(t + 1) * TT].rearrange("c p n -> p c n"), kst[:], "kst_w", [T_kst], [T_dst])

    def v_proj(t, hsrc, T_hs, slot):
        bsel = [2, 3, 4, 5]
        cnt = 0
        for pair in range(4):
            vbs = [2 * pair, 2 * pair + 1] if pair < 3 else [6]
            wbs = [wload("wv", vb, 2048) for vb in vbs]
            for j in range(4):
                bank = bsel[cnt % 4]
                cnt += 1
                for q_, (wb, tw) in enumerate(wbs):
                    for kc in range(8):
                        MM(ps[:, bank, q_ * 256:(q_ + 1) * 256], hsrc[:, kc, j * 128:(j + 1) * 128],
                           wb[:, kc * 256:(kc + 1) * 256], kc == 0, kc == 7, [tw, T_hs[kc]], [PB[bank]])
                ncol = 256 * len(vbs)
                CP("dve", vst[:, j, pair * 512:pair * 512 + ncol], ps[:, bank, 0:ncol], [PB[bank]], [T_vst])
                slot()
        DMA("pool", v_s[t * TT:(t + 1) * TT, :].rearrange("(j p) c -> p j c", p=128), vst[:], "vst_w", [T_vst], [T_v])

    def chain_thunks(t, own, h2dst, T_h2d):
        st = Stats()
        q = []
        for c in (6, 7):
            q.append(lambda c=c: TTo("pool", yT[:, c, :], yT[:, c, :], rs[:], ALU.mult, [T_yc[c], T_rs], [T_yc[c]]))
            q.append(lambda c=c: TTo("pool", xT[:, c, :], xT[:, c, :], yT[:, c, :], ALU.add, [T_xc[c], T_yc[c]], [T_xc[c]]))
        for c in range(6):
            k = c % 2
            q.append(lambda c=c, k=k: TTo("dve", tmpa[k][:], yT[:, c, :], rs[:], ALU.mult, [T_yc[c], T_rs], [T_tmpa[k]]))
            q.append(lambda c=c, k=k: TTo("dve", xT[:, c, :], xT[:, c, :], tmpa[k][:], ALU.add, [T_xc[c], T_tmpa[k]], [T_xc[c]]))
            q.append(lambda c=c: ACT(sq[:, c, :], xT[:, c, :], AF.Square, [T_xc[c]], [T_sqc[c]]))
            q.append(lambda c=c: st.push(c))
        for c in (6, 7):
            q.append(lambda c=c: ACT(sq[:, c, :], xT[:, c, :], AF.Square, [T_xc[c]], [T_sqc[c]]))
            q.append(lambda c=c: st.push(c))
        q.append(lambda: st.finish(D))
        for c in range(8):
            q.append(lambda c=c: STT("dve", h2dst[:, c, :], xT[:, c, :], gcols[:, 16 + c:17 + c], rs[:], ALU.mult, ALU.mult,
                                     [T_xc[c], T_rs, T_const], [T_h2d[c]]))
        if own:
            q.append(lambda: DMA("pool", x1_s[t], xT[:].rearrange("p c n -> p (c n)"), "x1_w", T_xc, [T_x1]))
            q.append(lambda: DMA("pool", h2_s[t], h2dst[:].rearrange("p c n -> p (c n)"), "h2_w", T_h2d, [T_h2]))
        return q

    for cb in cs2:
        P.add("dve", lambda e, cb=cb: e.memset(cb[:, 0, :], 1.0), w=[T_cs2[0], T_cs2[1]])
        P.add("dve", lambda e, cb=cb: e.memset(cb[:, 1, :], 0.0), w=[T_cs2[0], T_cs2[1]])
    a_tiles = list(cfg["A_tiles"]) if isinstance(cfg["A_tiles"], (list, tuple)) else list(range(cfg["A_tiles"]))
    prev = None

    def run_proj(pt, pown, pb, q):
        def slot():
            for _ in range(2):
                if q:
                    q.pop(0)()
        rope_proj(list(range(0, 14)), kT_s, T_kT, pt, h2T[pb], T_h2c[pb], cs2[pb], T_cs2[pb], slot)
        if pown:
            rope_proj(list(range(14, 28)), qT_s, T_qT, pt, h2T[pb], T_h2c[pb], cs2[pb], T_cs2[pb], slot)
        v_proj(pt, h2T[pb], T_h2c[pb], slot)
        while q:
            q.pop(0)()

    def x_load(tt):
        return DMA("sp", xtok[:], x_in[tt * TT:(tt + 1) * TT, :].rearrange("(j p) d -> p j d", p=128), "xl", [], [T_xtok])

    xl_op = x_load(a_tiles[0]) if a_tiles else None
    for ti, t in enumerate(a_tiles):
        own = t < 8
        bi = t % 2
        if t == a_tiles[0]:
            for nm, nb, xx in WSPEC:
                for b in range(nb):
                    if nm in ("wgu1", "wd1", "wrope", "wv"):
                        conv_block(nm, b, deps=[xl_op])
                    else:
                        conv_later.append((nm, b))
        else:
            for _ in range(3):
                if conv_later:
                    conv_block(*conv_later.pop(0))
        for hb_ in (0, 64):
            DMA("sp", cs2[bi][hb_:hb_ + 8, 0, :], cs_c[0, t], f"csl{bi}", [T_cs], [T_cs2[bi]])
            DMA("sp", cs2[bi][hb_ + 8:hb_ + 16, 0, :], cs_c[0, t], f"csl{bi}", [T_cs], [T_cs2[bi]])
            DMA("sp", cs2[bi][hb_:hb_ + 8, 1, :], cs_c[1, t], f"csl{bi}", [T_cs], [T_cs2[bi]])
            DMA("sp", cs2[bi][hb_ + 8:hb_ + 16, 1, :], cs_c[2, t], f"csl{bi}", [T_cs], [T_cs2[bi]])
        st0 = Stats()
        for c in range(8):
            bank = 6 + c % 2
            for j in range(4):
                TR(ps[:, bank, j * 128:(j + 1) * 128], xtok[:, j, c * 128:(c + 1) * 128], [T_xtok], [PB[bank]])
            CP("dve", xT[:, c, :], ps[:, bank, :], [PB[bank]], [T_xc[c]])
            ACT(sq[:, c, :], xT[:, c, :], AF.Square, [T_xc[c]], [T_sqc[c]])
            st0.push(c)
        st0.finish(D)
        h_from_x(0)
        nxt = (lambda tn=a_tiles[ti + 1]: x_load(tn)) if ti + 1 < len(a_tiles) else None
        ffn("wgu1", "wd1", ghalf, 0, mid=nxt)
        q = chain_thunks(t, own, h2T[bi], T_h2c[bi])
        if prev is not None:
            run_proj(prev[0], prev[1], prev[2], q)
        else:
            while q:
                q.pop(0)()
        prev = (t, own, bi)
    if prev is not None:
        run_proj(prev[0], prev[1], prev[2], [])

    while conv_later:
        conv_block(*conv_later.pop(0))
    P.barrier()
    A.off = setup_mark
    kd2 = [A.alloc([128, 2, 6144], BF16) for _ in range(2)]
    qd2 = [A.alloc([128, 2, SO], BF16) for _ in range(2)]
    vt = [A.alloc([128, 33, 256], BF16) for _ in range(2)]
    Uacc = A.alloc([128, 2, SO], F32)
    Lacc = A.alloc([128, 2, SO], F32)
    pef = [A.alloc([128, 2, 512], F32) for _ in range(2)]
    pm = [A.alloc([128, 1024], BF16) for _ in range(2)]
    odst = A.alloc([128, 2, 512], BF16)
    T_U, T_L, T_odst = (Tok() for _ in range(3))
    T_kd2 = [Tok(), Tok()]
    T_qd2 = [Tok(), Tok()]
    b2_groups = [(g_, (1, 4, 16)[g_]) for g_ in cfg["B2g"]] if cfg["B2"] else []

    def b2_group_loads(gi_):
        g_, _d = b2_groups[gi_]
        bb = gi_ % 2
        for cc_ in range(2):
            ci_ = 8 + 2 * g_ + cc_
            DMA("sp", kd2[bb][:, cc_, 0:1024], kT_s[ci_, :, S - 1024:S], f"kd_l{bb}", [T_kT], [T_kd2[bb]])
            DMA("sp", kd2[bb][:, cc_, 1024:6144], kT_s[ci_, :, 0:5120], f"kd_l{bb}", [T_kT], [T_kd2[bb]])
            DMA("sp", qd2[bb][:, cc_, :], qT_s[ci_, :, :], f"qd_l{bb}", [T_qT], [T_qd2[bb]])
    T_vt = [Tok(), Tok()]
    T_pef = [Tok(), Tok()]
    T_pm = [Tok(), Tok()]
    def B2_rest(sb, i, r, nq, vtt, tvt, Uv, Lv, g):
        ACT(pef[sb][:], ps[:, 2 * sb:2 * sb + 2, :], AF.Exp, [PB[2 * sb], PB[2 * sb + 1]], [T_pef[sb]], scale=0.125)
        mi = 1 if i == 0 else (2 if i == nq - 1 else 0)
        TTo("dve", pm[sb][:], pef[sb][:].rearrange("p a b -> p (a b)"), masks[:, mi, :], ALU.mult,
            [T_pef[sb], T_const], [T_pm[sb]])
        ub, lb = 4 + 2 * sb, 5 + 2 * sb
        if cfg["B2m"] < 3:
            return
        for j in range(4):
            cc, hp = j // 2, j % 2
            for m_ in range(2):
                blk = hp * 4 + m_ * 2 + cc
                MM(ps[hp * 64:(hp + 1) * 64, ub, cc * 128:(cc + 1) * 128],
                   vtt[:, i + m_, j * 64:(j + 1) * 64], pm[sb][:, blk * 128:(blk + 1) * 128],
                   m_ == 0, m_ == 1, [tvt, T_pm[sb]], [PB[ub]], tp=(0, hp * 64))
            for m_ in (range(2) if cfg["B2m"] >= 4 else []):
                blk = hp * 4 + m_ * 2 + cc
                MM(ps[hp * 64:(hp + 1) * 64, lb, cc * 128:(cc + 1) * 128],
                   ones_bf[:, 0:64], pm[sb][:, blk * 128:(blk + 1) * 128],
                   m_ == 0, m_ == 1, [T_const, T_pm[sb]], [PB[lb]], tp=(0, hp * 64))
        if cfg["B2m"] < 5:
            return
        usrc = ps[:, ub, 0:256].rearrange("p (c n) -> p c n", c=2)
        lsrc = ps[:, lb, 0:256].rearrange("p (c n) -> p c n", c=2)
        udst = Uv[:, :, 128 * i:128 * (i + 1), r]
        ldst = Lv[:, :, 128 * i:128 * (i + 1), r]
        if g == 0:
            CP("dve", udst, usrc, [PB[ub]], [T_U])
            CP("dve", ldst, lsrc, [PB[lb]], [T_L])
        else:
            TTo("dve", udst, udst, usrc, ALU.add, [PB[ub], T_U], [T_U])
            TTo("dve", ldst, ldst, lsrc, ALU.add, [PB[lb], T_L], [T_L])

    pend = [None]
    step = 0
    vcnt = 0
    for gi0 in range(min(2, len(b2_groups))):
        b2_group_loads(gi0)
    for gi, (g, dil) in enumerate(b2_groups):
        if gi >= 1 and gi + 1 < len(b2_groups):
            b2_group_loads(gi + 1)
        kd, qd = kd2[gi % 2], qd2[gi % 2]
        T_kd, T_qd = T_kd2[gi % 2], T_qd2[gi % 2]
        nq = SO // (128 * dil)
        kdv = kd[:].rearrange("p c (n d) -> p c n d", d=dil)
        qdv = qd[:].rearrange("p c (n d) -> p c n d", d=dil)
        Uv = Uacc[:].rearrange("p c (n d) -> p c n d", d=dil)
        Lv = Lacc[:].rearrange("p c (n d) -> p c n d", d=dil)
        colb = 1024 + g * 256
        for r in range(dil):
            vb = vcnt % 2
            vcnt += 1
            vtt, tvt = vt[vb], T_vt[vb]
            rs0 = v_s.tensor
            DMA("sp", vtt[0:64, 0, :], bass.AP(rs0, (S + r - 64 * dil) * 1792 + colb, [[dil * 1792, 64], [1, 256]]),
                f"vt{vb}", [T_v], [tvt])
            DMA("sp", vtt[64:128, 0, :], bass.AP(rs0, r * 1792 + colb, [[dil * 1792, 64], [1, 256]]),
                f"vt{vb}", [T_v], [tvt])
            m0 = 1
            while m0 <= nq:
                m1 = min(nq + 1, m0 + 8)
                DMA("sp", vtt[:, m0:m1, :],
                    bass.AP(rs0, (r + dil * (128 * m0 - 64)) * 1792 + colb,
                            [[dil * 1792, 128], [128 * dil * 1792, m1 - m0], [1, 256]]),
                    f"vt{vb}", [T_v], [tvt])
                m0 = m1
            for i in (range(nq) if cfg["B2c"] else []):
                sb = step % 2
                step += 1
                prev_rest = pend[0]
                for m_ in range(2):
                    kbase = (1024 // dil) + 128 * i - 64 + 128 * m_
                    for j in range(4):
                        cc, hp = j // 2, j % 2
                        blk = hp * 4 + m_ * 2 + cc
                        bank = 2 * sb + hp
                        col = (m_ * 2 + cc) * 128
                        MM(ps[:, bank, col:col + 128],
                           kdv[hp * 64:(hp + 1) * 64, cc, kbase:kbase + 128, r],
                           qdv[hp * 64:(hp + 1) * 64, cc, 128 * i:128 * (i + 1), r],
                           True, True, [T_kd, T_qd], [PB[2 * sb], PB[2 * sb + 1]])
                if prev_rest is not None:
                    prev_rest()

                def rest(sb=sb, i=i, r=r, nq=nq, vtt=vtt, tvt=tvt, Uv=Uv, Lv=Lv, g=g):
                    B2_rest(sb, i, r, nq, vtt, tvt, Uv, Lv, g)
                pend[0] = rest
    if pend[0] is not None:
        pend[0]()
    for q8 in (range(8) if cfg["B2"] else []):
        sl = slice(q8 * 512, (q8 + 1) * 512)
        RCP(Lacc[:, :, sl], Lacc[:, :, sl], [T_L], [T_L])
        TTo("dve", odst[:], Uacc[:, :, sl], Lacc[:, :, sl], ALU.mult, [T_U, T_L], [T_odst])
        DMA("pool", od_s[:, :, sl].rearrange("c p n -> p c n"), odst[:], "od_w", [T_odst], [T_od])

    P.barrier()
    A.off = setup_mark
    kT = [A.alloc([128, S], BF16) for _ in range(2)]
    vv = [A.alloc([128, 64, 128], BF16) for _ in range(2)]
    qT = [A.alloc([128, SO], BF16) for _ in range(2)]
    NP = 6
    pT = [A.alloc([128, 2, 512], BF16) for _ in range(NP)]
    lb1 = A.alloc([128, 512], F32)
    lb2 = A.alloc([128, 512], F32)
    acc1 = [A.alloc([128, 512], F32) for _ in range(2)]
    T_acc = [Tok(), Tok()]
    o1s = A.alloc([128, 512], F32)
    o2s = A.alloc([128, 512], F32)
    l2s = A.alloc([128, 512], F32)
    e1 = A.alloc([128, 512], F32)
    e2 = A.alloc([128, 512], F32)
    osb = A.alloc([128, 512], F32)
    sqo = A.alloc([128, 512], BF16)
    rso = A.alloc([128, 512], F32)
    oast = [A.alloc([128, 512], BF16) for _ in range(2)]
    T_kTb = [Tok(), Tok()]
    T_vv = [Tok(), Tok()]
    T_qTb = [Tok(), Tok()]
    T_pT = [Tok() for _ in range(NP)]
    T_o1s, T_o2s, T_l2s, T_lb1, T_lb2, T_e1, T_e2, T_osb, T_sqo, T_rso = (Tok() for _ in range(10))
    T_oast = [Tok(), Tok()]
    gstep = 0
    ecnt = 0
    epi_q = []

    def epi_pop():
        if epi_q:
            f = epi_q.pop(0)
            if f is not None:
                f()

    for h in range(cfg["B1_heads"]):
        hb = h % 2
        DMA("sp", kT[hb][:], kT_s[h], f"kT{hb}", [T_kT], [T_kTb[hb]])
        DMA("sp", qT[hb][:], qT_s[h], f"qT{hb}", [T_qT], [T_qTb[hb]])
        vsrc = v_s[:, h * 128:(h + 1) * 128].rearrange("(kt p) c -> p kt c", p=128)
        for k4 in range(4):
            DMA("sp", vv[hb][:, k4 * 16:(k4 + 1) * 16, :], vsrc[:, k4 * 16:(k4 + 1) * 16, :], f"vv{hb}", [T_v], [T_vv[hb]])
        for qc in range(cfg["B1_qc"]):
            qsl = slice(qc * 512, (qc + 1) * 512)
            ab = ecnt % 2

            def QK(kt, sb):
                ksl = slice(kt * 128, (kt + 1) * 128)
                MM(ps[:, 2 * sb, :], kT[hb][0:64, ksl], qT[hb][0:64, qsl], True, True,
                   [T_kTb[hb], T_qTb[hb]], [PB[2 * sb], PB[2 * sb + 1]])
                MM(ps[:, 2 * sb + 1, :], kT[hb][64:128, ksl], qT[hb][64:128, qsl], True, True,
                   [T_kTb[hb], T_qTb[hb]], [PB[2 * sb], PB[2 * sb + 1]])

            QK(0, gstep % 2)
            QK(1, (gstep + 1) % 2)
            for kt in range(64):
                sb = gstep % 2
                pi = gstep % NP
                gstep += 1
                ACT(pT[pi][:], ps[:, 2 * sb:2 * sb + 2, :], AF.Exp, [PB[2 * sb], PB[2 * sb + 1]], [T_pT[pi]], scale=0.125)
                if kt + 2 < 64:
                    QK(kt + 2, sb)
                first, last = kt == 0, kt == 63
                MM(ps[:, 4, :], vv[hb][:, kt, :], pT[pi][:, 0, :], first, last, [T_vv[hb], T_pT[pi]], [PB[4]])
                MM(ps[:, 5, :], vv[hb][:, kt, :], pT[pi][:, 1, :], first, last, [T_vv[hb], T_pT[pi]], [PB[5]])
                MM(ps[:, 7, :], ones_bf[:], pT[pi][:, 1, :], first, last, [T_const, T_pT[pi]], [PB[7]])
                if first:
                    CP("dve", acc1[ab][:], pT[pi][:, 0, :], [T_pT[pi]], [T_acc[ab]])
                else:
                    TTo("dve", acc1[ab][:], acc1[ab][:], pT[pi][:, 0, :], ALU.add, [T_pT[pi], T_acc[ab]], [T_acc[ab]])
                epi_pop()
            while epi_q:
                epi_pop()
            CP("dve", o1s[:], ps[:, 4, :], [PB[4]], [T_o1s])
            CP("dve", o2s[:], ps[:, 5, :], [PB[5]], [T_o2s])
            CP("dve", l2s[:], ps[:, 7, :], [PB[7]], [T_l2s])
            MM(ps[:, 6, :], ones_f[:], acc1[ab][:], True, True, [T_acc[ab], T_const], [PB[6]])
            ob = ecnt % 2
            ecnt += 1

            def mk(h=h, qsl=qsl, ob=ob):
                return [
                    lambda: RCP(lb1[:], ps[:, 6, :], [PB[6]], [T_lb1]),
                    lambda: RCP(lb2[:], l2s[:], [T_l2s], [T_lb2]),
                    lambda: TTo("dve", e1[:], o1s[:], lb1[:], ALU.mult, [T_o1s, T_lb1], [T_e1]),
                    lambda: TTo("dve", e2[:], o2s[:], lb2[:], ALU.mult, [T_o2s, T_lb2], [T_e2]),
                    lambda: STT("dve", osb[:], e2[:], neglam[:, 0:1], e1[:], ALU.mult, ALU.add, [T_e1, T_e2, T_const], [T_osb]),
                    lambda: TTo("dve", sqo[:], osb[:], osb[:], ALU.mult, [T_osb], [T_sqo]),
                    None, None,
                    lambda: MM(ps[:, 6, :], ones_bf[:], sqo[:], True, True, [T_sqo, T_const], [PB[6]]),
                    None, None,
                    lambda: ACT(rso[:], ps[:, 6, :], AF.Ln, [PB[6]], [T_rso], scale=1.0 / 128, bias=EPS),
                    None,
                    lambda: ACT(rso[:], rso[:], AF.Exp, [T_rso], [T_rso], scale=-0.5),
                    None,
                    lambda: STT("dve", oast[ob][:], osb[:], gsub8[:, 0:1], rso[:], ALU.mult, ALU.mult,
                                [T_osb, T_rso, T_const], [T_oast[ob]]),
                    lambda: DMA("pool", oa_s[h, :, qsl], oast[ob][:], f"oa_w{ob}", [T_oast[ob]], [T_oa]),
                ]
            epi_q.extend(mk())
    while epi_q:
        epi_pop()

    P.barrier()
    A.off = setup_mark
    xtok = A.alloc([128, 4, 1024], F32)
    xT = A.alloc([128, 8, TT], F32)
    hT = A.alloc([128, 8, TT], BF16)
    sq = A.alloc([128, 8, TT], BF16)
    aT = A.alloc([128, NFC, TT], BF16)
    yT = A.alloc([128, 8, TT], F32)
    rs = A.alloc([128, TT], F32)
    tmpa = [A.alloc([128, TT], F32) for _ in range(2)]
    sg = [A.alloc([128, TT], F32) for _ in range(2)]
    wpool = [A.alloc([128, 4096], BF16) for _ in range(NW)]
    oaTb = [A.alloc([128, 8, TT], BF16) for _ in range(2)]
    odTb = [A.alloc([128, 2, TT], BF16) for _ in range(2)]
    h2Lb = [A.alloc([128, 8, TT], BF16) for _ in range(2)]
    T_oaTb, T_odTb, T_h2Lb = [Tok(), Tok()], [Tok(), Tok()], [Tok(), Tok()]
    mg = A.alloc([128, 8, TT], BF16)
    wpbt = A.alloc([128, 2048], BF16)
    twpb = Tok()
    DMA("sp", wpbt[:], w_s["wpb"][0], "wpb_l", w_blk_all["wpb"], [twpb])
    outs = []
    tmpc = [A.alloc([128, TT], F32) for _ in range(2)]
    T_tmpc = [Tok(), Tok()]
    NCT = cfg["C_tiles"]
    wo_pref = [None]

    def c_loads(tt):
        b_ = tt % 2
        sl_ = slice(tt * TT, (tt + 1) * TT)
        DMA("sp", oaTb[b_][:], oa_s[:, :, sl_].rearrange("h p n -> p h n"), f"oal{b_}", [T_oa], [T_oaTb[b_]])
        DMA("sp", odTb[b_][:], od_s[:, :, sl_].rearrange("c p n -> p c n"), f"odl{b_}", [T_od], [T_odTb[b_]])
        DMA("sp", h2Lb[b_][:].rearrange("p c n -> p (c n)"), h2_s[tt], f"h2l{b_}", [T_h2], [T_h2Lb[b_]])

    def c_proj(tt, slot):
        oaT, odT, h2L = oaTb[tt % 2], odTb[tt % 2], h2Lb[tt % 2]
        T_oaT, T_odT, T_h2L = T_oaTb[tt % 2], T_odTb[tt % 2], T_h2Lb[tt % 2]
        for o in range(8):
            bset = (2, 3, 4, 5) if o % 2 == 0 else (6, 7, 0, 5)
            if o % 2 == 0:
                wpa_b = wload("wpa", o // 2, 2048)
                wga_b = wload("wg", o // 2, 2048)
                wgb_b = wload("wg", 4 + o // 2, 2048)
            co = (o % 2) * 128
            for kc in range(8):
                MM(ps[:, bset[0], :], wpa_b[0][:, kc * 256 + co:kc * 256 + co + 128], oaT[:, kc, :], kc == 0, kc == 7,
                   [wpa_b[1], T_oaT], [PB[bset[0]]])
            for kc in range(8):
                MM(ps[:, bset[1], :], wga_b[0][:, kc * 256 + co:kc * 256 + co + 128], h2L[:, kc, :], kc == 0, kc == 7,
                   [wga_b[1], T_h2L], [PB[bset[1]]])
            for kc in range(2):
                MM(ps[:, bset[2], :], wpbt[:, kc * 1024 + o * 128:kc * 1024 + (o + 1) * 128], odT[:, kc, :], kc == 0, kc == 1,
                   [twpb, T_odT], [PB[bset[2]]])
            for kc in range(8):
                MM(ps[:, bset[3], :], wgb_b[0][:, kc * 256 + co:kc * 256 + co + 128], h2L[:, kc, :], kc == 0, kc == 7,
                   [wgb_b[1], T_h2L], [PB[bset[3]]])
            ACT(sg[0][:], ps[:, bset[1], :], AF.Sigmoid, [PB[bset[1]]], [T_sg[0]])
            TTo("dve", tmpa[0][:], sg[0][:], ps[:, bset[0], :], ALU.mult, [T_sg[0], PB[bset[0]]], [T_tmpa[0]])
            ACT(sg[1][:], ps[:, bset[3], :], AF.Sigmoid, [PB[bset[3]]], [T_sg[1]])
            TTo("dve", tmpa[1][:], sg[1][:], ps[:, bset[2], :], ALU.mult, [T_sg[1], PB[bset[2]]], [T_tmpa[1]])
            TTo("pool", mg[:, o, :], tmpa[0][:], tmpa[1][:], ALU.add, [T_tmpa[0], T_tmpa[1]], [T_mg])
            slot()

    def c_chain1():
        st = Stats()
        q = []
        for c in (6, 7):
            q.append(lambda c=c: TTo("pool", yT[:, c, :], yT[:, c, :], rs[:], ALU.mult, [T_yc[c], T_rs], [T_yc[c]]))
            q.append(lambda c=c: TTo("pool", xT[:, c, :], xT[:, c, :], yT[:, c, :], ALU.add, [T_xc[c], T_yc[c]], [T_xc[c]]))
        for c in range(6):
            k = c % 2
            q.append(lambda c=c, k=k: TTo("dve", tmpc[k][:], yT[:, c, :], rs[:], ALU.mult, [T_yc[c], T_rs], [T_tmpc[k]]))
            q.append(lambda c=c, k=k: TTo("dve", xT[:, c, :], xT[:, c, :], tmpc[k][:], ALU.add, [T_xc[c], T_tmpc[k]], [T_xc[c]]))
            q.append(lambda c=c: ACT(sq[:, c, :], xT[:, c, :], AF.Square, [T_xc[c]], [T_sqc[c]]))
            q.append(lambda c=c: st.push(c))
        for c in (6, 7):
            q.append(lambda c=c: ACT(sq[:, c, :], xT[:, c, :], AF.Square, [T_xc[c]], [T_sqc[c]]))
            q.append(lambda c=c: st.push(c))
        q.append(lambda: st.finish(D))
        for c in range(8):
            q.append(lambda c=c: STT("dve", hT[:, c, :], xT[:, c, :], gcols[:, 32 + c:33 + c], rs[:], ALU.mult, ALU.mult,
                                     [T_xc[c], T_rs, T_const], [T_hc[c]]))
        return q

    if NCT > 0:
        c_loads(0)
        if NCT > 1:
            c_loads(1)
        c_proj(0, lambda: None)
    for t in range(NCT):
        tsl = slice(t * TT, (t + 1) * TT)
        if t + 2 < NCT:
            c_loads(t + 2)
        sto = Stats()
        if wo_pref[0] is None:
            wo_pref[0] = [wload("wo", b_, 2048) for b_ in range(4)]
        wo_blocks = wo_pref[0]
        wo_pref[0] = None
        for o in range(8):
            if o % 2 == 0:
                wo_b = wo_blocks[o // 2]
            if o == 6:
                DMA("sp", xT[:].rearrange("p c n -> p (c n)"), x1_s[t], "x1l", [T_x1, T_oa, T_od], T_xc)
            co = (o % 2) * 128
            bank = 6 + o % 2
            for kc in range(8):
                MM(ps[:, bank, :], wo_b[0][:, kc * 256 + co:kc * 256 + co + 128], mg[:, kc, :], kc == 0, kc == 7,
                   [wo_b[1], T_mg], [PB[bank]])
            evac_y(o, bank, gcols, 24, sto)
        sto.finish(D)
        q = c_chain1()
        if t + 1 < NCT:
            def slot():
                for _ in range(6):
                    if q:
                        q.pop(0)()
            c_proj(t + 1, slot)
        while q:
            q.pop(0)()
        ffn("wgu2", "wd2", ghalf, 8)
        resid_apply(None)
        if t + 1 < NCT:
            wo_pref[0] = [wload("wo", b_, 2048) for b_ in range(4)]
        cnt = 0
        for j in range(4):
            for c4 in range(2):
                bank = 6 + cnt % 2
                cnt += 1
                for cq in range(4):
                    c = c4 * 4 + cq
                    TR(ps[:, bank, cq * 128:(cq + 1) * 128], xT[:, c, j * 128:(j + 1) * 128], [T_xc[c]], [PB[bank]])
                CP("dve", xtok[:, j, c4 * 512:(c4 + 1) * 512], ps[:, bank, :], [PB[bank]], [T_xtok])
        outs.append(DMA("sp", out_ap[tsl, :].rearrange("(j p) d -> p j d", p=128), xtok[:], "out_w", [T_xtok], []))

    if dbg:
        P.barrier()
        dbg_o = nc.dram_tensor("dbg_o", [128, 128], F32, kind="ExternalOutput").ap()
        outs.append(DMA("sp", dbg_o, ident[:], "dbg_w", [T_const], []))
    P.emit(outs)
    return nc


def _blocks_km(W, cb):
    K, N = W.shape
    r = W.reshape(K // 128, 128, N // cb, cb).transpose(2, 1, 0, 3)
    return np.ascontiguousarray(r.reshape(N // cb, 128, (K // 128) * cb))


_NC_CACHE = {}


def prep_inputs(x, positions, w_in, lambda_q1, lambda_k1, lambda_q2, lambda_k2, g_subln,
                w_proj_a, w_proj_b, w_out, w_gu1, w_down1, w_gu2, w_down2,
                g_pre_ffn1, g_post_ffn1, g_pre_mix, g_post_mix, g_pre_ffn2, g_post_ffn2):
    f32 = np.float32
    x = np.asarray(x, f32)
    positions = np.asarray(positions, np.int32)
    w_in = np.asarray(w_in, f32)[0]

    def gu_blocks(W):
        W = np.asarray(W, f32)[0]
        r = W.reshape(8, 128, 2, 11, 256).transpose(3, 1, 0, 2, 4)
        return np.ascontiguousarray(r.reshape(11, 128, 4096))

    def down_blocks(W):
        W = np.asarray(W, f32)[0]
        r = W.reshape(22, 128, 8, 128).transpose(2, 1, 0, 3)
        return np.ascontiguousarray(r.reshape(8, 128, 2816))

    col_ka, col_kd, col_qa, col_qd = 1024, 3072 + 768, 0, 3072
    chunk_cols = ([col_ka + 128 * i for i in range(8)] + [col_kd + 128 * i for i in range(6)]
                  + [col_qa + 128 * i for i in range(8)] + [col_qd + 128 * i for i in range(6)])
    partner = np.arange(128)
    for hb in (0, 64):
        partner[hb:hb + 8] = np.arange(hb + 8, hb + 16)
        partner[hb + 8:hb + 16] = np.arange(hb, hb + 8)
    wrope = np.empty((28, 128, 8, 2, 128), f32)
    for i, c0 in enumerate(chunk_cols):
        Wc = w_in[:, c0:c0 + 128].reshape(8, 128, 128).transpose(1, 0, 2)
        wrope[i, :, :, 0, :] = Wc
        wrope[i, :, :, 1, :] = Wc[:, :, partner]
    wrope = wrope.reshape(28, 128, 2048)
    wv = _blocks_km(np.concatenate([w_in[:, 2048:3072], w_in[:, 3072 + 1536:3072 + 2304]], axis=1), 256)
    wg = _blocks_km(w_in[:, 5376:7424], 256)
    wpa = _blocks_km(np.asarray(w_proj_a, f32)[0], 256)
    wo = _blocks_km(np.asarray(w_out, f32)[0], 256)
    wpb = _blocks_km(np.asarray(w_proj_b, f32)[0], 1024)
    weights = {"wgu1": gu_blocks(w_gu1), "wd1": down_blocks(w_down1), "wrope": wrope, "wv": wv, "wg": wg,
               "wpa": wpa, "wpb": wpb, "wo": wo, "wgu2": gu_blocks(w_gu2), "wd2": down_blocks(w_down2)}

    gl = [g_pre_ffn1, g_post_ffn1, g_pre_mix, g_post_mix, g_pre_ffn2, g_post_ffn2]
    gcols = np.concatenate([np.asarray(g, f32)[0].reshape(8, 128).T for g in gl], axis=1)
    gcols = np.ascontiguousarray(gcols)
    gsub = np.ascontiguousarray(np.asarray(g_subln, f32)[0].reshape(128, 1))
    lam4 = np.concatenate([np.asarray(v, f32)[0] for v in (lambda_q1, lambda_k1, lambda_q2, lambda_k2)])
    lam4 = np.ascontiguousarray(np.broadcast_to(lam4[None, :], (128, 256)))
    ident = np.eye(128, dtype=f32)
    pm = np.arange(128) % 64
    inv = (THETA ** (-(np.arange(128) % 8) / 8.0)).astype(f32)
    sgn = np.where(pm < 8, -1.0, np.where(pm < 16, 1.0, 0.0)).astype(f32)
    invsgn = np.ascontiguousarray(np.stack([inv, sgn], axis=1))
    kk = np.arange(128)[:, None]
    qq = np.arange(128)[None, :]
    lo = (kk >= qq).astype(f32)
    hi = (kk <= qq).astype(f32)

    def mask_tile(lo_m, hi_m):
        return np.concatenate([lo_m, lo_m, hi_m, hi_m, lo_m, lo_m, hi_m, hi_m], axis=1)

    in_maps = []
    for c in range(8):
        b, half = c // 2, c % 2
        T0 = half * SO
        lo_first = lo.copy()
        hi_last = hi.copy()
        if half == 0:
            lo_first[0:64, :] = 0.0
        else:
            hi_last[64:128, :] = 0.0
        masks = np.stack([mask_tile(lo, hi), mask_tile(lo_first, hi), mask_tile(lo, hi_last)], axis=1)
        m = {
            "x_loc": np.ascontiguousarray(np.roll(x[b], -T0, axis=0)),
            "pos_gf": np.ascontiguousarray(np.repeat(np.roll(positions[b], -T0).reshape(16, 1, 512), 8, axis=1).reshape(128, 512)),
            "ident": ident, "invsgn": invsgn, "masks": np.ascontiguousarray(masks.reshape(128, 3072)),
            "gcols": gcols, "gsub": gsub, "lam4": lam4,
        }
        m.update(weights)
        in_maps.append(m)
    return in_maps


def kernel(**inputs):
    f32 = np.float32
    in_maps = prep_inputs(**inputs)
    if "nc" not in _NC_CACHE:
        _NC_CACHE["nc"] = build_program()
    nc = _NC_CACHE["nc"]
    res = run_bass_kernel_spmd(nc, in_maps, core_ids=list(range(8)))
    out = np.empty((4, S, D), f32)
    for c in range(8):
        b, half = c // 2, c % 2
        out[b, half * SO:(half + 1) * SO, :] = res.results[c]["out"]
    return out
```
